# Optimizing a Trainium2 kernel written in Bass

```python
import math
import jax, jax.numpy as jnp
from jax import lax
import numpy as np

D_MODEL = 4096
BATCH = 4
SEQ = 4096
DEPTH = 1

CONV_DIM = D_MODEL // 2
CONV_GROUPS = 16
CONV_WIDTH = 3
N_HEADS = 16
QK_NOPE_DIM = 128
QK_ROPE_DIM = 64
V_HEAD_DIM = 128
QK_HEAD_DIM = QK_NOPE_DIM + QK_ROPE_DIM
ATTN_DIM = N_HEADS * V_HEAD_DIM
Q_LORA_RANK = 1024
KV_LORA_RANK = 512
N_BRANCHES = 2
D_FF = ((8 * D_MODEL // 3 + 255) // 256) * 256
ROPE_THETA = 10000.0
RMS_EPS = 1e-6
Q_BLOCK = 128
SOFTMAX_SCALE = 1.0 / math.sqrt(QK_HEAD_DIM)

IN_SPLITS = (CONV_DIM, CONV_DIM, CONV_DIM, Q_LORA_RANK, KV_LORA_RANK, QK_ROPE_DIM,
             N_BRANCHES * D_MODEL)
IN_COLS = CONV_DIM * 3 + Q_LORA_RANK + KV_LORA_RANK + QK_ROPE_DIM + N_BRANCHES * D_MODEL

kernel_name = "hybrid_shortconv_mla_gated_encoder"


def split_columns(z, widths):
    parts = []
    start = 0
    for wdt in widths:
        parts.append(z[..., start:start + wdt])
        start += wdt
    return parts


def rms_norm(x, g):
    xf = x.astype(jnp.float32)
    inv = lax.rsqrt(jnp.mean(xf * xf, axis=-1, keepdims=True) + RMS_EPS)
    return (xf * inv * g.astype(jnp.float32)).astype(x.dtype)


def centred_short_conv(u, w):
    up = jnp.pad(u, ((0, 0), (1, 1), (0, 0)))
    return up[:, :-2] * w[0] + up[:, 1:-1] * w[1] + up[:, 2:] * w[2]


def rotary(t, cos, sin):
    half = t.shape[-1] // 2
    t1, t2 = t[..., :half], t[..., half:]
    return jnp.concatenate([t1 * cos - t2 * sin, t1 * sin + t2 * cos], axis=-1)


def mla_attention(q_nope, q_rope, k_nope, k_rope, v):
    b, s, h, _ = q_nope.shape
    nblk = s // Q_BLOCK

    def to_blocks(t):
        return jnp.swapaxes(t.reshape((b, nblk, Q_BLOCK) + t.shape[2:]), 0, 1)

    def block(qs):
        qn, qr = qs
        sc = jnp.einsum('bqhd,bkhd->bhqk', qn, k_nope).astype(jnp.float32)
        sc = sc + jnp.einsum('bqhr,bkr->bhqk', qr, k_rope).astype(jnp.float32)
        p = jax.nn.softmax(sc * SOFTMAX_SCALE, axis=-1).astype(v.dtype)
        return jnp.einsum('bhqk,bkhd->bqhd', p, v)

    out = lax.map(block, (to_blocks(q_nope), to_blocks(q_rope)))
    return jnp.swapaxes(out, 0, 1).reshape(b, s, h * V_HEAD_DIM)


def setup_inputs(seed: int = 0) -> dict:
    key = jax.random.key(seed)
    ks = jax.random.split(key, 20)

    def w(k, shape, fan_in):
        return jax.random.normal(k, shape, jnp.float32) * (fan_in ** -0.5)

    def gain(k, shape):
        return 1.0 + 0.02 * jax.random.normal(k, shape, jnp.float32)

    x = jax.random.normal(ks[0], (BATCH, SEQ, D_MODEL), jnp.float32)
    positions = (jnp.arange(SEQ, dtype=jnp.int32)[None, :]
                 + jax.random.randint(ks[1], (BATCH, 1), 0, 1024, dtype=jnp.int32))
    return {
        "x": x,
        "positions": positions,
        "g_mix": gain(ks[2], (DEPTH, D_MODEL)),
        "w_in": w(ks[3], (DEPTH, D_MODEL, IN_COLS), D_MODEL),
        "b_gate": 0.01 * jax.random.normal(ks[4], (DEPTH, N_BRANCHES * D_MODEL), jnp.float32),
        "conv_w": w(ks[5], (DEPTH, CONV_WIDTH, CONV_DIM), CONV_WIDTH),
        "g_q_a": gain(ks[6], (DEPTH, Q_LORA_RANK)),
        "w_q_b": w(ks[7], (DEPTH, Q_LORA_RANK, N_HEADS * QK_HEAD_DIM), Q_LORA_RANK),
        "g_kv_a": gain(ks[8], (DEPTH, KV_LORA_RANK)),
        "w_kv_b": w(ks[9], (DEPTH, KV_LORA_RANK, N_HEADS * (QK_NOPE_DIM + V_HEAD_DIM)), KV_LORA_RANK),
        "w_branch": w(ks[10], (DEPTH, N_BRANCHES, CONV_DIM, D_MODEL), CONV_DIM),
        "w_out": w(ks[11], (DEPTH, D_MODEL, D_MODEL), D_MODEL),
        "g_ffn": gain(ks[12], (DEPTH, D_MODEL)),
        "w_ffn_gate": w(ks[13], (DEPTH, D_MODEL, D_FF), D_MODEL),
        "w_ffn_up": w(ks[14], (DEPTH, D_MODEL, D_FF), D_MODEL),
        "w_ffn_down": w(ks[15], (DEPTH, D_FF, D_MODEL), D_FF),
        "g_final": gain(ks[16], (D_MODEL,)),
    }


def reference(x, positions, g_mix, w_in, b_gate, conv_w, g_q_a, w_q_b, g_kv_a, w_kv_b,
              w_branch, w_out, g_ffn, w_ffn_gate, w_ffn_up, w_ffn_down, g_final):
    b, s, d = x.shape
    dt = x.dtype
    inv_freq = ROPE_THETA ** (-jnp.arange(0, QK_ROPE_DIM, 2, dtype=jnp.float32) / QK_ROPE_DIM)
    ang = positions.astype(jnp.float32)[..., None] * inv_freq[None, None, :]
    cos, sin = jnp.cos(ang).astype(dt), jnp.sin(ang).astype(dt)

    for l in range(DEPTH):
        h = rms_norm(x, g_mix[l])
        z = h @ w_in[l]
        c_b, c_c, c_h, q_a, kv_a, k_rope, z_gate = split_columns(z, IN_SPLITS)

        y_a = c_b * centred_short_conv(c_c * c_h, conv_w[l])

        q = (rms_norm(q_a, g_q_a[l]) @ w_q_b[l]).reshape(b, s, N_HEADS, QK_HEAD_DIM)
        q_nope, q_rope = q[..., :QK_NOPE_DIM], q[..., QK_NOPE_DIM:]
        q_rope = rotary(q_rope, cos[:, :, None, :], sin[:, :, None, :])
        kv = (rms_norm(kv_a, g_kv_a[l]) @ w_kv_b[l]).reshape(b, s, N_HEADS, QK_NOPE_DIM + V_HEAD_DIM)
        k_nope, v = kv[..., :QK_NOPE_DIM], kv[..., QK_NOPE_DIM:]
        k_rope = rotary(k_rope, cos, sin)
        y_b = mla_attention(q_nope, q_rope, k_nope, k_rope, v)

        y_br = jnp.einsum('nbsc,ncd->bsnd', jnp.stack([y_a, y_b], axis=0), w_branch[l])
        gates = jax.nn.sigmoid((z_gate + b_gate[l]).astype(jnp.float32)).astype(dt)
        merged = jnp.sum(gates.reshape(b, s, N_BRANCHES, d) * y_br, axis=2)
        x = x + merged @ w_out[l]

        h2 = rms_norm(x, g_ffn[l])
        x = x + (jax.nn.silu(h2 @ w_ffn_gate[l]) * (h2 @ w_ffn_up[l])) @ w_ffn_down[l]

    return rms_norm(x, g_final)
```

```python
import math
import sys
from contextlib import ExitStack

import numpy as np
import concourse.bass as bass
import concourse.mybir as mybir
from concourse.bass_utils import run_bass_kernel_spmd

F32 = mybir.dt.float32
BF16 = mybir.dt.bfloat16
I32 = mybir.dt.int32
AF = mybir.ActivationFunctionType
ALU = mybir.AluOpType
AX = mybir.AxisListType

RMS_EPS = 1e-6
ROPE_THETA = 10000.0
TWO_PI = 2.0 * math.pi
PI_SAFE = 3.1415925


DEBUG_LINES = False
LINE_MAP = {}


class Cfg:
    def __init__(s, D=4096, S=4096, NOWN=2048, CONV=2048, H=16, QL=1024, KVL=512, DFF=11008):
        s.D, s.S, s.NOWN, s.CONV, s.H, s.QL, s.KVL, s.DFF = D, S, NOWN, CONV, H, QL, KVL, DFF
        s.KC = D // 128
        s.NT = NOWN // 512
        s.NA = S // 512
        s.CC = CONV // 128
        s.QC = QL // 128
        s.KVC = KVL // 128
        s.FC = DFF // 128
        s.SCALE = 1.0 / math.sqrt(192.0)
        o = 0
        s.c_gmix = o; o += s.KC
        s.c_bgate = o; o += 2 * s.KC
        s.c_convw = o; o += 3 * s.CC
        s.c_gqa = o; o += s.QC
        s.c_gkva = o; o += s.KVC
        s.c_gffn = o; o += s.KC
        s.c_invf = o; o += 1
        s.c_sign = o; o += 1
        s.c_eps = o; o += 1
        s.c_zero = o; o += 1
        s.NCOLS = o


class _Stop(Exception):
    pass


STOP_AFTER = None
EVAC_MODE = 0


def _ckpt(name):
    if STOP_AFTER == name:
        raise _Stop()


class Buf:
    __slots__ = ("w", "r", "war", "name", "excl")

    def __init__(self, name="", excl=False):
        self.w = {}
        self.r = {}
        self.war = {}
        self.name = name
        self.excl = excl


def _merge(d, src):
    for k, (sem, v) in src.items():
        cur = d.get(k)
        if cur is None or cur[1] < v:
            d[k] = (sem, v)


class Eng:
    def __init__(self, name, sem):
        self.name = name
        self.sem = sem
        self.cnt = 0
        self.seen = {}
        self.stream = []


class DSem:
    def __init__(self, key, sem):
        self.key = key
        self.sem = sem
        self.cnt = 0


class Arena:
    def __init__(self, tensor, nbytes):
        self.t = tensor
        self.nbytes = nbytes
        self.live = []

    def buf(self, lo, nbytes, name=""):
        hi = lo + nbytes
        assert hi <= self.nbytes, (name, lo, nbytes, self.nbytes)
        b = Buf(name)
        keep = []
        for (l, h, ob) in self.live:
            if l < hi and lo < h:
                _merge(b.r, ob.w)
                _merge(b.r, ob.r)
                _merge(b.r, ob.war)
                if not (lo <= l and h <= hi):
                    keep.append((l, h, ob))
            else:
                keep.append((l, h, ob))
        keep.append((lo, hi, b))
        self.live = keep
        return b

    def ap(self, lo, nbytes, dtype, pattern=None, **kw):
        assert lo % 4 == 0 and nbytes % 4 == 0
        a = self.t[:, lo // 4:(lo + nbytes) // 4]
        if dtype != F32:
            a = a.bitcast(dtype)
        if pattern is not None:
            a = a.rearrange(pattern, **kw)
        return a


class K:
    def __init__(self, nc, cfg):
        self.nc = nc
        self.cfg = cfg
        self.engs = {}
        self.nsem = 0

    def new_eng(self, name, sem):
        e = Eng(name, sem)
        self.engs[name] = e
        return e

    def _wait(self, eng, deps):
        for key, (sem, val) in deps.items():
            if eng.name == "pe" and key == "pe":
                continue
            if eng.seen.get(key, 0) < val:
                eng.stream.append(("w", sem, val))
                eng.seen[key] = val

    def op(self, eng, fn, reads=(), writes=(), pwrites=(), inc=True):
        deps = {}
        for b in reads:
            _merge(deps, b.w)
            if b.excl:
                _merge(deps, {kk_: vv_ for kk_, vv_ in b.r.items() if kk_ != eng.name})
        for b in writes:
            _merge(deps, b.w)
            _merge(deps, b.r)
        for b in pwrites:
            _merge(deps, b.r)
            _merge(deps, b.war)
        self._wait(eng, deps)
        if inc:
            eng.cnt += 1
            val = eng.cnt
        else:
            val = eng.cnt + 1
        eng.stream.append(("i", fn, inc, sys._getframe(1).f_lineno if DEBUG_LINES else 0))
        tok = {eng.name: (eng.sem, val)}
        for b in reads:
            _merge(b.r, tok)
        for b in writes:
            nw = dict(b.w)
            _merge(nw, b.r)
            b.war = nw
            b.w = dict(tok)
            b.r = {}
        for b in pwrites:
            _merge(b.w, tok)
        return tok

    def dma(self, q, dsem, out, in_, reads=(), writes=(), pwrites=()):
        deps = {}
        for b in reads:
            _merge(deps, b.w)
        for b in writes:
            _merge(deps, b.w)
            _merge(deps, b.r)
        for b in pwrites:
            _merge(deps, b.r)
            _merge(deps, b.war)
        self._wait(q, deps)
        dsem.cnt += 16
        q.stream.append(("d", out, in_, dsem.sem))
        tok = {dsem.key: (dsem.sem, dsem.cnt)}
        for b in reads:
            _merge(b.r, tok)
        for b in writes:
            nw = dict(b.w)
            _merge(nw, b.r)
            b.war = nw
            b.w = dict(tok)
            b.r = {}
        for b in pwrites:
            _merge(b.w, tok)
        return tok


def _I(name, *args, **kwargs):
    return (name, args, kwargs)


def _replay(e, eng):
    for item in eng.stream:
        kind = item[0]
        if kind == "w":
            e.wait_ge(item[1], item[2])
        elif kind == "i":
            name, args, kwargs = item[1]
            ins = getattr(e, name)(*args, **kwargs)
            if DEBUG_LINES:
                try:
                    LINE_MAP[str(ins.ins.name)] = item[3]
                except Exception:
                    pass
            if item[2]:
                ins.then_inc(eng.sem, 1)
        else:
            e.dma_start(out=item[1], in_=item[2]).then_inc(item[3], 16)


def build(cfg):
    nc = bass.Bass("TRN2", target_bir_lowering=False)
    c = cfg
    D, S, KC, CC, QC, KVC, FC, H = c.D, c.S, c.KC, c.CC, c.QC, c.KVC, c.FC, c.H
    CONV, QL, KVL, DFF = c.CONV, c.QL, c.KVL, c.DFF
    NKT = S // 128
    NKB = S // 512

    def din(name, shape, dt=F32):
        return nc.dram_tensor(name, list(shape), dt, kind="ExternalInput").ap()

    x_seq = din("x_seq", [S, D])
    x_own = din("x_own", [c.NT * 514, D])
    pos_seq = din("pos_seq", [64, S], I32)
    pos_own = din("pos_own", [64, c.NOWN], I32)
    class WS:
        def __init__(self, name, rows, cols):
            kcf, ncols = slab_geometry(c)[name]
            self.kcf, self.ncols = kcf, ncols
            nr = (rows + kcf * 128 - 1) // (kcf * 128)
            assert cols % ncols == 0
            self.ap = din(name, [nr, cols // ncols, 128, kcf * ncols])

    w_a = WS("w_a", D, 3 * CONV + QL)
    w_kvin = WS("w_kvin", D, KVL + 256)
    w_gt = WS("w_gt", D, 2 * D)
    w_qb = WS("w_qb", QL, H * 384)
    w_kvb = WS("w_kvb", KVL, H * 256)
    w_br = WS("w_br", 2 * CONV, D)
    w_out = WS("w_out", D, D)
    w_g = WS("w_g", D, DFF)
    w_u = WS("w_u", D, DFF)
    w_d = WS("w_d", DFF, D)
    cols_d = din("cols", [128, c.NCOLS])
    gfin_d = din("gfin", [128, D])
    ident_d = din("ident", [128, 128])
    out_d = nc.dram_tensor("out", [c.NOWN, D], F32, kind="ExternalOutput").ap()
    kscr = nc.dram_tensor("kscr", [H, 128, S], BF16, kind="Internal").ap()
    vscr = nc.dram_tensor("vscr", [H, 128, S], BF16, kind="Internal").ap()

    def al(v, a=256):
        return (v + a - 1) // a * a
    HT_SZ = KC * 514 * 2
    OFF1 = al(HT_SZ)
    XST = D * 4
    YB_SZ = CC * 1024
    KH_SZ = S * 2
    OFF_YB = OFF1
    OFF_KH = OFF_YB + YB_SZ
    OFF_VH = OFF_KH + KH_SZ
    OFF2 = max(OFF1 + 2 * XST, OFF_VH + KH_SZ, 4 * D * 4)
    YA_SZ = max(CC * 1024, 16384)
    OFF3 = OFF2 + YA_SZ
    QN_SZ = QC * 1024
    OFF_RQ = OFF3 + QN_SZ
    OFF4 = OFF_RQ + 2048
    TMP_SZ = 26624
    ARENA = max(OFF4 + TMP_SZ, OFF3 + KC * 1024, OFF2 + 51200)
    if 2 * KH_SZ >= 16384:
        P3_G_OFF = OFF_KH
    else:
        P3_G_OFF = ARENA
        ARENA += 16384
    SLOT = 16384
    NSLOT = 3

    es = ExitStack()
    with es:
        def sb(name, shape, dt):
            return es.enter_context(nc.sbuf_tensor(name, list(shape), dt))

        arena_t = sb("arena", [128, ARENA // 4], F32)
        ring_t = sb("ring", [128, NSLOT * SLOT // 2], BF16)
        kvnT_t = sb("kvnT", [128, KVC, S], BF16)
        krT_t = sb("krT", [128, S], BF16)
        cols_t = sb("colsb", [128, c.NCOLS], F32)
        ident_t = sb("identb", [128, 128], BF16)
        ones_t = sb("onesb", [128, 128], BF16)
        stat_t = sb("stat", [128, 64], F32)
        psum_t = [es.enter_context(nc.psum_tensor("ps%d" % i, [128, 512], F32)) for i in range(8)]

        def sem(name):
            return es.enter_context(nc.semaphore(name))

        k = K(nc, c)
        PE = k.new_eng("pe", sem("s_pe"))
        ACT = k.new_eng("act", sem("s_act"))
        DVE = k.new_eng("dve", sem("s_dve"))
        POOL = k.new_eng("pool", sem("s_pool"))
        SP = k.new_eng("sp", sem("s_sp"))

        all_dsems = []

        def dsem(name):
            d = DSem(name, sem(name))
            all_dsems.append(d)
            return d

        arena = Arena(arena_t, ARENA)

        def col(off, n=1, p=128):
            return cols_t[0:p, off:off + n]

        banks = [(psum_t[i], Buf("bank%d" % i, excl=True)) for i in range(8)]
        free_banks = list(range(8))

        def bank_get():
            return free_banks.pop(0)

        def bank_put(i):
            free_banks.append(i)

        def bk(i):
            return psum_t[i][:]

        def bkb(i):
            return banks[i][1]

        cbuf = Buf("consts")
        ds_const = dsem("d_const")
        k.dma(SP, ds_const, cols_t[:], cols_d, writes=[cbuf])
        ds_const2 = dsem("d_const2")
        k.dma(POOL, ds_const2, ident_t[:], ident_d, pwrites=[cbuf])
        onesb = Buf("ones")
        k.op(DVE, _I('memset', ones_t[:], 1.0), writes=[onesb])
        krbuf_z = Buf("krz")
        k.op(DVE, _I('memset', krT_t[:], 0.0), writes=[krbuf_z])
        statb = [Buf("stat%d" % i) for i in range(64)]

        ring = []
        for i in range(NSLOT):
            ring.append((Buf("slot%d" % i), dsem("d_slot%d" % i)))
        ring_pos = [0]

        def load_slab(w_ap, r0, kc, c0, ncols):
            assert kc * ncols * 2 <= SLOT
            i = ring_pos[0] % NSLOT
            ring_pos[0] += 1
            b, ds = ring[i]
            base = i * (SLOT // 2)
            ap = ring_t[:, base:base + kc * ncols].rearrange("p (k n) -> p k n", k=kc)
            ws = w_ap
            assert r0 % (ws.kcf * 128) == 0 and c0 % ws.ncols == 0 and kc <= ws.kcf and ncols == ws.ncols, \
                (r0, kc, c0, ncols, ws.kcf, ws.ncols)
            src = ws.ap[r0 // (ws.kcf * 128), c0 // ws.ncols][:, 0:kc * ncols].rearrange("p (k n) -> p k n", k=kc)
            k.dma(POOL, ds, ap, src, writes=[b])
            return ap, b

        def mm_group(bank, out_ap, pairs, reads, partial=False):
            n = len(pairs)
            tok = None
            for j, (l, r) in enumerate(pairs):
                last = (j == n - 1)
                tok = k.op(PE, (_I('matmul', out_ap, lhsT=l, rhs=r, start=(j == 0), stop=last)),
                           reads=reads if j == 0 else (),
                           writes=() if (partial or j > 0) else [bkb(bank)],
                           pwrites=[bkb(bank)] if (partial or j > 0) else (),
                           inc=last)
            return tok

        rr = [0]

        def evac_eng():
            rr[0] += 1
            if EVAC_MODE == 1:
                return ACT
            if EVAC_MODE == 2:
                return DVE
            return ACT if rr[0] % 2 else DVE

        def scaled_copy(eng, out, in_, scale_ap, reads, writes=(), pwrites=()):
            if eng is ACT:
                return k.op(ACT, _I('activation', out=out, in_=in_, func=AF.Copy, scale=scale_ap),
                            reads=reads, writes=writes, pwrites=pwrites)
            return k.op(DVE, _I('tensor_scalar', out=out, in0=in_, scalar1=scale_ap, scalar2=None,
                                                        op0=ALU.mult),
                        reads=reads, writes=writes, pwrites=pwrites)

        xst = [(arena_t, OFF1 + i * XST) for i in range(2)]
        ds_x = [dsem("d_x0"), dsem("d_x1")]
        xcnt = [0]

        def x_prefetch(src_rows, nrows):
            i = xcnt[0] % 2
            xcnt[0] += 1
            xb = arena.buf(xst[i][1], XST, "xst")
            xap = arena.ap(xst[i][1], XST, F32)
            k.dma(SP, ds_x[i], xap[0:nrows, :], src_rows, writes=[xb])
            return (i, xb, xap)

        def rows_stage1(pre, xn_offs):
            i, xb, xap = pre
            xn_off = xn_offs[i]
            stat_i = 4 * i
            xnb = arena.buf(xn_off, D * 2, "xn")
            xn = arena.ap(xn_off, D * 2, BF16)
            ssq = stat_t[:, stat_i:stat_i + 1]
            std = stat_t[:, stat_i + 1:stat_i + 2]
            rstd = stat_t[:, stat_i + 2:stat_i + 3]
            sb_ = statb[stat_i]
            k.op(ACT, _I('activation', out=xn, in_=xap, func=AF.Square, accum_out=ssq),
                 reads=[xb], writes=[xnb, sb_])
            k.op(ACT, _I('activation', out=std, in_=ssq, func=AF.Sqrt, bias=col(c.c_eps),
                         scale=1.0 / D), reads=[sb_, cbuf], pwrites=[sb_])
            k.op(DVE, _I('reciprocal', out=rstd, in_=std), reads=[sb_], pwrites=[sb_])
            k.op(DVE, _I('tensor_scalar', out=xn, in0=xap, scalar1=rstd, scalar2=None, op0=ALU.mult),
                 reads=[xb, sb_], writes=[xnb])
            return (xn, xnb)

        def rows_stage2(s1, hT_ap, hT_bufs, col0, ncol, gcol):
            xn, xnb = s1
            for g in range(0, KC, 8):
                bi = bank_get()
                pb = bk(bi).bitcast(BF16)
                ng = min(8, KC - g)
                for j in range(ng):
                    kk = g + j
                    k.op(PE, (_I('transpose', out=pb[:, j * 128:(j + 1) * 128],
                                 in_=xn[:, kk * 128:(kk + 1) * 128], identity=ident_t[:])),
                         reads=[xnb, cbuf] if j == 0 else (),
                         writes=[bkb(bi)] if j == 0 else (), pwrites=() if j == 0 else [bkb(bi)],
                         inc=(j == ng - 1))
                eng_ = evac_eng()
                for j in range(ng):
                    kk = g + j
                    scaled_copy(eng_, hT_ap[:, kk, col0:col0 + ncol], pb[:, j * 128:j * 128 + ncol],
                                col(gcol + kk), reads=[bkb(bi), cbuf], pwrites=[hT_bufs[kk]])
                bank_put(bi)

        ds_pos = dsem("d_pos")

        def trig_tables(pos_ap, off, cc_off, ss_off):
            pb_ = arena.buf(off, 2048, "posi")
            posi = arena.ap(off, 2048, I32)[0:64, :]
            ab = arena.buf(off + 2048, 2048, "ang")
            ang = arena.ap(off + 2048, 2048, F32)[0:64, :]
            tb = arena.buf(off + 4096, 2048, "trt")
            tt = arena.ap(off + 4096, 2048, F32)[0:64, :]
            ib = arena.buf(off + 6144, 2048, "tri")
            ti = arena.ap(off + 6144, 2048, I32)[0:64, :]
            ccb = arena.buf(cc_off, 2048, "cct")
            cct = arena.ap(cc_off, 2048, F32)[0:64, :]
            ssb = arena.buf(ss_off, 2048, "sst")
            sst = arena.ap(ss_off, 2048, F32)[0:64, :]
            k.dma(SP, ds_pos, posi, pos_ap, writes=[pb_])
            V = DVE
            k.op(V, _I('tensor_copy', out=ang, in_=posi), reads=[pb_], writes=[ab])
            k.op(V, _I('tensor_scalar', out=ang, in0=ang, scalar1=col(c.c_invf, 1, 64), scalar2=None,
                                              op0=ALU.mult), reads=[cbuf], writes=[ab])

            def reduce_to(dst, dstb, shift):
                k.op(V, _I('tensor_scalar', out=tt, in0=ang, scalar1=1.0 / TWO_PI,
                                                  scalar2=0.5 + shift / TWO_PI, op0=ALU.mult, op1=ALU.add),
                     reads=[ab], writes=[tb])
                k.op(V, _I('tensor_copy', out=ti, in_=tt), reads=[tb], writes=[ib])
                k.op(V, _I('tensor_copy', out=tt, in_=ti), reads=[ib], writes=[tb])
                C1 = 6.28125
                C2 = TWO_PI - 6.28125
                k.op(V, _I('scalar_tensor_tensor', out=dst, in0=tt, scalar=-C1, in1=ang,
                                                         op0=ALU.mult, op1=ALU.add), reads=[tb, ab], writes=[dstb])
                k.op(V, _I('scalar_tensor_tensor', out=dst, in0=tt, scalar=-C2, in1=dst,
                                                         op0=ALU.mult, op1=ALU.add), reads=[tb], writes=[dstb])
                if shift != 0.0:
                    k.op(V, _I('tensor_scalar', out=dst, in0=dst, scalar1=shift, scalar2=None,
                                                      op0=ALU.add), writes=[dstb])
                k.op(V, _I('tensor_scalar', out=tt, in0=dst, scalar1=-math.pi, scalar2=TWO_PI,
                                                  op0=ALU.is_lt, op1=ALU.mult), reads=[dstb], writes=[tb])
                k.op(V, _I('tensor_tensor', out=dst, in0=dst, in1=tt, op=ALU.add), reads=[tb], writes=[dstb])
                k.op(V, _I('tensor_scalar', out=tt, in0=dst, scalar1=math.pi, scalar2=-TWO_PI,
                                                  op0=ALU.is_gt, op1=ALU.mult), reads=[dstb], writes=[tb])
                k.op(V, _I('tensor_tensor', out=dst, in0=dst, in1=tt, op=ALU.add), reads=[tb], writes=[dstb])
                k.op(V, _I('tensor_scalar', out=dst, in0=dst, scalar1=PI_SAFE, scalar2=-PI_SAFE,
                                                  op0=ALU.min, op1=ALU.max), writes=[dstb])

            reduce_to(sst, ssb, 0.0)
            reduce_to(cct, ccb, math.pi / 2)
            k.op(ACT, _I('activation', out=sst, in_=sst, func=AF.Sin, scale=col(c.c_sign, 1, 64)),
                 reads=[cbuf], writes=[ssb])
            k.op(ACT, _I('activation', out=cct, in_=cct, func=AF.Sin), writes=[ccb])
            return cct, ccb, sst, ssb

        def stat_begin(off):
            return {"bank": bank_get(), "n": 0, "off": off}

        def stat_add(st, bi, last):
            off = st["off"]
            sbk = st["bank"]
            sqb = arena.buf(off, 2048, "sq")
            sq = arena.ap(off, 2048, F32)
            hib = arena.buf(off + 2048, 1024, "hi")
            hi = arena.ap(off + 2048, 1024, BF16)
            lob = arena.buf(off + 3072, 1024, "lo")
            lo = arena.ap(off + 3072, 1024, BF16)
            first = (st["n"] == 0)
            st["n"] += 1
            k.op(ACT, _I('activation', out=sq, in_=bk(bi), func=AF.Square),
                 reads=[bkb(bi)], writes=[sqb])
            k.op(DVE, _I('tensor_copy', out=hi, in_=sq), reads=[sqb], writes=[hib])
            k.op(DVE, _I('tensor_tensor', out=lo, in0=sq, in1=hi, op=ALU.subtract),
                 reads=[sqb, hib], writes=[lob])
            k.op(PE, _I('matmul', bk(sbk), lhsT=ones_t[:], rhs=hi, start=first, stop=False),
                 reads=[hib, onesb], writes=[bkb(sbk)] if first else (),
                 pwrites=() if first else [bkb(sbk)], inc=False)
            k.op(PE, _I('matmul', bk(sbk), lhsT=ones_t[:], rhs=lo, start=False, stop=last),
                 reads=[lob], pwrites=[bkb(sbk)], inc=True)

        def stat_finish(st, n_feat, out_ap, outb):
            sbk = st["bank"]
            k.op(ACT, _I('activation', out=out_ap, in_=bk(sbk), func=AF.Sqrt, bias=col(c.c_eps),
                                             scale=1.0 / n_feat), reads=[bkb(sbk), cbuf], writes=[outb])
            bank_put(sbk)
            k.op(DVE, _I('reciprocal', out=out_ap, in_=out_ap), writes=[outb])

        def retire(bufs, b0):
            for b in bufs:
                if b is not b0:
                    _merge(b0.r, b.r)
                    _merge(b0.r, b.w)

        hT = arena.ap(0, HT_SZ, BF16, "p (k n) -> p k n", k=KC)

        def new_hT_bufs():
            b = arena.buf(0, HT_SZ, "hT")
            return [b] + [Buf("hT%d" % i) for i in range(1, KC)], b

        def inherit(bufs, b0):
            for b in bufs[1:]:
                b.r = dict(b0.r)

        kvnb = [[Buf("kvn") for _ in range(NKB)] for _ in range(KVC)]
        krb = [Buf("kr") for _ in range(NKB)]
        for b in krb:
            b.w = dict(krbuf_z.w)

        ds_xr = [dsem("d_xr%d" % i) for i in range(4)]
        ds_st = [dsem("d_st%d" % i) for i in range(4)]
        ds_gf = dsem("d_gf")
        try:
            A_XN = OFF2
            A_TRIG = OFF2 + 16384
            A_CC = A_TRIG + 8192
            A_SS = A_CC + 2048
            A_SQ = A_SS + 2048
            A_RSTD = A_SQ + 4096
            A_T1 = A_RSTD + 2048
            A_T2 = A_T1 + 2048
            A_KST = A_TRIG
            A_VST = A_T2 + 2048
            assert A_VST + 8192 <= ARENA
            kst_cnt = [0]
            ds_kst = [dsem("d_kst0"), dsem("d_kst1")]
            ds_vst = [dsem("d_vst0"), dsem("d_vst1")]
            kscrb = [[Buf("kscr") for _ in range(c.NA)] for _ in range(max(1, H // 4))]
            vscrb = [[Buf("vscr") for _ in range(c.NA)] for _ in range(max(1, H // 4))]
            ds_kh = [dsem("d_kh0"), dsem("d_kh1")]
            ds_vh = [dsem("d_vh0"), dsem("d_vh1")]
            _ckpt('C')
            HG = min(4, H)

            def kv_prod_group(a, g):
                kslab, ksb = load_slab(w_kvb, 0, KVC, g * HG * 128, HG * 128)
                si_ = kst_cnt[0] % 2
                kst_cnt[0] += 1
                kstb = arena.buf(A_KST + si_ * 4096, 4096, "kst")
                kst = arena.ap(A_KST + si_ * 4096, HG * 1024, BF16, "p (h n) -> p h n", h=HG)
                for hl in range(HG):
                    bi = bank_get()
                    mm_group(bi, bk(bi), [(kslab[:, cc_, hl * 128:(hl + 1) * 128],
                                           kvnT_t[:, cc_, a * 512:(a + 1) * 512]) for cc_ in range(KVC)],
                             reads=[ksb] + [kvnb[cc_][a] for cc_ in range(KVC)])
                    if evac_eng() is ACT:
                        k.op(ACT, _I('activation', out=kst[:, hl, :], in_=bk(bi), func=AF.Copy),
                             reads=[bkb(bi)], writes=[kstb] if hl == 0 else (), pwrites=() if hl == 0 else [kstb])
                    else:
                        k.op(DVE, _I('tensor_copy', out=kst[:, hl, :], in_=bk(bi)),
                             reads=[bkb(bi)], writes=[kstb] if hl == 0 else (), pwrites=() if hl == 0 else [kstb])
                    bank_put(bi)
                k.dma(SP, ds_kst[si_], kscr[g * HG:(g + 1) * HG, :, a * 512:(a + 1) * 512].rearrange("h p n -> p h n"),
                      kst, reads=[kstb], writes=[kscrb[g][a]])
                vslab, vsb = load_slab(w_kvb, 0, KVC, H * 128 + g * HG * 128, HG * 128)
                vstb = arena.buf(A_VST + si_ * 4096, 4096, "vst")
                vst = arena.ap(A_VST + si_ * 4096, HG * 1024, BF16, "p (h k d) -> p h k d", h=HG, k=4)
                for q4 in range(4):
                    kt = a * 4 + q4
                    bi = bank_get()
                    mm_group(bi, bk(bi)[:, 0:HG * 128],
                             [(kvnT_t[:, cc_, kt * 128:(kt + 1) * 128], vslab[:, cc_, :]) for cc_ in range(KVC)],
                             reads=[vsb] + [kvnb[cc_][a] for cc_ in range(KVC)])
                    src = bk(bi)[:, 0:HG * 128].rearrange("p (h d) -> p h d", h=HG)
                    if evac_eng() is ACT:
                        k.op(ACT, _I('activation', out=vst[:, :, q4, :], in_=src, func=AF.Copy),
                             reads=[bkb(bi)], writes=[vstb] if q4 == 0 else (), pwrites=() if q4 == 0 else [vstb])
                    else:
                        k.op(DVE, _I('tensor_copy', out=vst[:, :, q4, :], in_=src),
                             reads=[bkb(bi)], writes=[vstb] if q4 == 0 else (), pwrites=() if q4 == 0 else [vstb])
                    bank_put(bi)
                k.dma(SP, ds_vst[si_], vscr[g * HG:(g + 1) * HG, :, a * 512:(a + 1) * 512].rearrange(
                    "h p (k d) -> p h k d", k=4), vst, reads=[vstb], writes=[vscrb[g][a]])

            nrt_tot = S // 128
            A_XNS = [A_XN, A_XN + 8192]
            pres = {0: x_prefetch(x_seq[0:128, :], 128)}
            if nrt_tot > 1:
                pres[1] = x_prefetch(x_seq[128:256, :], 128)
            s1s = {0: rows_stage1(pres[0], A_XNS)}
            for a in range(c.NA):
                hb, hb0 = new_hT_bufs()
                inherit(hb, hb0)
                for rt in range(4):
                    idx = a * 4 + rt
                    if idx + 2 < nrt_tot:
                        pres[idx + 2] = x_prefetch(x_seq[(idx + 2) * 128:(idx + 3) * 128, :], 128)
                    if idx + 1 < nrt_tot:
                        s1s[idx + 1] = rows_stage1(pres[idx + 1], A_XNS)
                    rows_stage2(s1s[idx], hT, hb, rt * 128, 128, c.c_gmix)
                    if rt == 3:
                        early_slab = load_slab(w_kvin, 0, KC, 0, 256)
                    if a >= 1 and rt < H // HG:
                        kv_prod_group(a - 1, rt)
                _ckpt('A%da' % a)
                cct, ccb, sst, ssb = trig_tables(pos_seq[:, a * 512:(a + 1) * 512], A_TRIG, A_CC, A_SS)
                _ckpt('A%db' % a)
                chunk_bank = []
                ncols_tot = KVL + 256
                nchunks = ncols_tot // 128
                ci = 0
                for c0 in range(0, ncols_tot, 256):
                    ncol = min(256, ncols_tot - c0)
                    slab, sbuf_ = early_slab if c0 == 0 else load_slab(w_kvin, 0, KC, c0, ncol)
                    for j in range(ncol // 128):
                        bi = bank_get()
                        mm_group(bi, bk(bi), [(slab[:, kk, j * 128:(j + 1) * 128], hT[:, kk, 0:512]) for kk in range(KC)],
                                 reads=[sbuf_] + hb)
                        chunk_bank.append(bi)
                        ci += 1
                _ckpt('A%dc' % a)
                rb = arena.buf(A_RSTD, 2048, "rstdkv")
                rstd_kv = arena.ap(A_RSTD, 2048, F32)
                st = stat_begin(A_SQ)
                for j in range(KVC):
                    stat_add(st, chunk_bank[j], j == KVC - 1)
                stat_finish(st, KVL, rstd_kv, rb)
                for j in range(KVC):
                    bi = chunk_bank[j]
                    k.op(DVE, _I('scalar_tensor_tensor',
                        out=kvnT_t[:, j, a * 512:(a + 1) * 512], in0=bk(bi), scalar=col(c.c_gkva + j),
                        in1=rstd_kv, op0=ALU.mult, op1=ALU.mult),
                        reads=[bkb(bi), rb, cbuf], writes=[kvnb[j][a]])
                    bank_put(bi)
                _ckpt('A%dd' % a)
                bA, bB = chunk_bank[KVC], chunk_bank[KVC + 1]
                t1b = arena.buf(A_T1, 2048, "t1")
                t1 = arena.ap(A_T1, 2048, F32)[0:64, :]
                t2b = arena.buf(A_T2, 2048, "t2")
                t2 = arena.ap(A_T2, 2048, F32)[0:64, :]
                k.op(DVE, _I('tensor_tensor', out=t1, in0=bk(bA)[0:64, :], in1=cct, op=ALU.mult),
                     reads=[bkb(bA), ccb], writes=[t1b])
                k.op(DVE, _I('tensor_tensor', out=t2, in0=bk(bB)[0:64, :], in1=sst, op=ALU.mult),
                     reads=[bkb(bB), ssb], writes=[t2b])
                k.op(DVE, _I('tensor_tensor', out=krT_t[0:64, a * 512:(a + 1) * 512], in0=t1, in1=t2, op=ALU.add),
                     reads=[t1b, t2b], writes=[krb[a]])
                bank_put(bA)
                bank_put(bB)
                retire(hb, hb0)
                _ckpt('A%d' % a)

            for g in range(H // HG):
                kv_prod_group(c.NA - 1, g)

            P1_XN = OFF4
            P1_CC = OFF4 + 8192
            P1_U = P1_CC + 4608
            P1_YC = P1_U + 4608
            P1_SQ = P1_YC + 2048
            assert P1_SQ + 4096 <= ARENA
            P2_CC = OFF4
            P2_SS = OFF4 + 2048
            P2_QN = OFF4 + 4096
            P2_QR = P2_QN + 2048
            P2_RL = P2_QR + 2048
            P2_PT = P2_RL + 2048
            P2_RT = P2_PT + 4096
            P2_ACC = P2_RT + 6144
            P2_HL = P2_ACC + 4096
            assert P2_HL + 2048 <= ARENA
            P3_G = P3_G_OFF
            OFF_MG = OFF3
            P4_XR = OFF2
            P5_XN = OFF2
            P5_ACT = OFF2 + 8192
            PF_G = OFF3
            PF_J = OFF3 + D * 4
            assert PF_J + D * 2 <= ARENA
            xrcnt = [0]
            FG = 8

            for tj in range(c.NT):
                tb0 = tj * 514
                hb, hb0 = new_hT_bufs()
                inherit(hb, hb0)
                srcs = [(x_own[tb0 + rt * 128:tb0 + rt * 128 + 128, :], 128, rt * 128, 128) for rt in range(4)]
                srcs.append((x_own[tb0 + 512:tb0 + 514, :], 2, 512, 2))
                P1_XNS = [P1_XN, P1_XN + 8192]
                pr_ = {0: x_prefetch(srcs[0][0], srcs[0][1]), 1: x_prefetch(srcs[1][0], srcs[1][1])}
                st_ = {0: rows_stage1(pr_[0], P1_XNS)}
                for ri, (src_, nr_, c0_, nc_) in enumerate(srcs):
                    if ri + 2 < len(srcs):
                        pr_[ri + 2] = x_prefetch(srcs[ri + 2][0], srcs[ri + 2][1])
                    if ri + 1 < len(srcs):
                        st_[ri + 1] = rows_stage1(pr_[ri + 1], P1_XNS)
                    rows_stage2(st_[ri], hT, hb, c0_, nc_, c.c_gmix)

                yab0 = arena.buf(OFF2, YA_SZ, "yaT")
                yab = [yab0] + [Buf("ya") for _ in range(CC - 1)]
                inherit(yab, yab0)
                yaT = arena.ap(OFF2, CC * 1024, BF16, "p (k n) -> p k n", k=CC)
                for blk in range(CONV // 256):
                    ccs, us = [], []
                    slab, sbuf_ = load_slab(w_a, 0, KC, CONV + blk * 256, 256)
                    for j in range(2):
                        bi = bank_get()
                        bh = bank_get()
                        mm_group(bi, bk(bi), [(slab[:, kk, j * 128:(j + 1) * 128], hT[:, kk, 0:512]) for kk in range(KC)],
                                 reads=[sbuf_] + hb)
                        mm_group(bh, bk(bh)[:, 0:2], [(slab[:, kk, j * 128:(j + 1) * 128], hT[:, kk, 512:514])
                                                      for kk in range(KC)], reads=[sbuf_] + hb)
                        cb_ = arena.buf(P1_CC + j * 2304, 2304, "ccsb")
                        cs = arena.ap(P1_CC + j * 2304, 2056, F32)
                        k.op(ACT, _I('activation', out=cs[:, 1:513], in_=bk(bi), func=AF.Copy),
                             reads=[bkb(bi)], writes=[cb_])
                        k.op(ACT, _I('activation', out=cs[:, 0:1], in_=bk(bh)[:, 0:1], func=AF.Copy),
                             reads=[bkb(bh)], pwrites=[cb_])
                        k.op(ACT, _I('activation', out=cs[:, 513:514], in_=bk(bh)[:, 1:2], func=AF.Copy),
                             reads=[bkb(bh)], pwrites=[cb_])
                        bank_put(bi)
                        bank_put(bh)
                        ccs.append((cs, cb_))
                    slab, sbuf_ = load_slab(w_a, 0, KC, 2 * CONV + blk * 256, 256)
                    for j in range(2):
                        bi = bank_get()
                        bh = bank_get()
                        mm_group(bi, bk(bi), [(slab[:, kk, j * 128:(j + 1) * 128], hT[:, kk, 0:512]) for kk in range(KC)],
                                 reads=[sbuf_] + hb)
                        mm_group(bh, bk(bh)[:, 0:2], [(slab[:, kk, j * 128:(j + 1) * 128], hT[:, kk, 512:514])
                                                      for kk in range(KC)], reads=[sbuf_] + hb)
                        cs, cb_ = ccs[j]
                        ub = arena.buf(P1_U + j * 2304, 2304, "u")
                        u = arena.ap(P1_U + j * 2304, 2056, F32)
                        k.op(DVE, _I('tensor_tensor', out=u[:, 1:513], in0=bk(bi), in1=cs[:, 1:513],
                                                                                op=ALU.mult),
                             reads=[bkb(bi), cb_], writes=[ub])
                        k.op(DVE, _I('tensor_tensor', out=u[:, 0:1], in0=bk(bh)[:, 0:1],
                                                                                in1=cs[:, 0:1], op=ALU.mult),
                             reads=[bkb(bh), cb_], pwrites=[ub])
                        k.op(DVE, _I('tensor_tensor', out=u[:, 513:514], in0=bk(bh)[:, 1:2],
                                                                                in1=cs[:, 513:514], op=ALU.mult),
                             reads=[bkb(bh), cb_], pwrites=[ub])
                        bank_put(bi)
                        bank_put(bh)
                        us.append((u, ub))
                    slab, sbuf_ = load_slab(w_a, 0, KC, blk * 256, 256)
                    for j in range(2):
                        ch = blk * 2 + j
                        bi = bank_get()
                        mm_group(bi, bk(bi), [(slab[:, kk, j * 128:(j + 1) * 128], hT[:, kk, 0:512]) for kk in range(KC)],
                                 reads=[sbuf_] + hb)
                        u, ub = us[j]
                        ycb = arena.buf(P1_YC, 2048, "yc")
                        yc = arena.ap(P1_YC, 2048, F32)
                        w0 = col(c.c_convw + 0 * CC + ch)
                        w1 = col(c.c_convw + 1 * CC + ch)
                        w2 = col(c.c_convw + 2 * CC + ch)
                        k.op(DVE, _I('tensor_scalar', out=yc, in0=u[:, 1:513], scalar1=w1, scalar2=None,
                                                                         op0=ALU.mult), reads=[ub, cbuf], writes=[ycb])
                        k.op(DVE, _I('scalar_tensor_tensor', out=yc, in0=u[:, 0:512], scalar=w0, in1=yc,
                                                                                op0=ALU.mult, op1=ALU.add),
                             reads=[ub], writes=[ycb])
                        k.op(DVE, _I('scalar_tensor_tensor', out=yc, in0=u[:, 2:514], scalar=w2, in1=yc,
                                                                                op0=ALU.mult, op1=ALU.add),
                             reads=[ub], writes=[ycb])
                        k.op(DVE, _I('tensor_tensor', out=yaT[:, ch, :], in0=bk(bi), in1=yc, op=ALU.mult),
                             reads=[bkb(bi), ycb], writes=[yab[ch]])
                        bank_put(bi)

                qnb0 = arena.buf(OFF3, QN_SZ, "qnT")
                qnb = [qnb0] + [Buf("qn") for _ in range(QC - 1)]
                inherit(qnb, qnb0)
                qnT = arena.ap(OFF3, QN_SZ, BF16, "p (k n) -> p k n", k=QC)
                rqb = arena.buf(OFF_RQ, 2048, "rstdq")
                rstdq = arena.ap(OFF_RQ, 2048, F32)
                st = stat_begin(P1_SQ)
                for blk in range(QL // 256):
                    slab, sbuf_ = load_slab(w_a, 0, KC, 3 * CONV + blk * 256, 256)
                    for j in range(2):
                        ch = blk * 2 + j
                        bi = bank_get()
                        mm_group(bi, bk(bi), [(slab[:, kk, j * 128:(j + 1) * 128], hT[:, kk, 0:512]) for kk in range(KC)],
                                 reads=[sbuf_] + hb)
                        scaled_copy(evac_eng(), qnT[:, ch, :], bk(bi), col(c.c_gqa + ch), reads=[bkb(bi), cbuf],
                                    writes=[qnb[ch]])
                        stat_add(st, bi, ch == QC - 1)
                        bank_put(bi)
                stat_finish(st, QL, rstdq, rqb)
                _ckpt('P1')
                cct, ccb, sst, ssb = trig_tables(pos_own[:, tj * 512:(tj + 1) * 512], P2_PT, P2_CC, P2_SS)
                ybb0 = arena.buf(OFF_YB, YB_SZ, "ybT")
                ybb = [ybb0] + [Buf("yb") for _ in range(H - 1)]
                inherit(ybb, ybb0)
                ybT = arena.ap(OFF_YB, YB_SZ, BF16, "p (k n) -> p k n", k=CC)
                KhT = arena.ap(OFF_KH, KH_SZ, BF16)
                Vh = arena.ap(OFF_VH, KH_SZ, BF16, "p (k n) -> p k n", k=NKT)
                ptb = [arena.buf(P2_PT + i * 1024, 1024, "PT") for i in range(4)]
                ptap = [arena.ap(P2_PT + i * 1024, 1024, BF16) for i in range(4)]
                rtb = [arena.buf(P2_RT + i * 2048, 2048, "rt") for i in range(3)]
                rtap = [arena.ap(P2_RT + i * 2048, 2048, F32)[0:64, :] for i in range(3)]
                accb = [arena.buf(P2_ACC + i * 2048, 2048, "acc") for i in range(2)]
                accap = [arena.ap(P2_ACC + i * 2048, 2048, F32) for i in range(2)]
                ahib = arena.buf(P2_HL, 1024, "ahi")
                ahi = arena.ap(P2_HL, 1024, BF16)
                alob = arena.buf(P2_HL + 1024, 1024, "alo")
                alo = arena.ap(P2_HL + 1024, 1024, BF16)
                rlb = arena.buf(P2_RL, 2048, "rl")
                rl = arena.ap(P2_RL, 2048, F32)
                qhnb = [arena.buf(P2_QN + i * 1024, 1024, "qhn") for i in range(2)]
                qhn = [arena.ap(P2_QN + i * 1024, 1024, BF16) for i in range(2)]
                qhrb = [arena.buf(P2_QR + i * 1024, 1024, "qhr") for i in range(2)]
                qhr = [arena.ap(P2_QR + i * 1024, 1024, BF16) for i in range(2)]
                for i in range(2):
                    k.op(DVE, _I('memset', qhr[i], 0.0), writes=[qhrb[i]])
                ptc = 0
                HGc = min(4, H)
                khf = [arena.buf(OFF_KH + i * (KH_SZ // 2), KH_SZ // 2, "khalf") for i in range(2)]
                vhf = [arena.buf(OFF_VH + i * (KH_SZ // 2), KH_SZ // 2, "vhalf") for i in range(2)]
                qs_state = {}

                def emit_kv(h, i):
                    k.dma(SP, ds_kh[i], KhT[:, i * (S // 2):(i + 1) * (S // 2)],
                          kscr[h, :, i * (S // 2):(i + 1) * (S // 2)],
                          reads=[kscrb[h // HGc][a_] for a_ in range(i * c.NA // 2, (i + 1) * c.NA // 2)],
                          writes=[khf[i]])
                    k.dma(SP, ds_vh[i], Vh[:, i * (NKT // 2):(i + 1) * (NKT // 2), :],
                          vscr[h, :, i * (S // 2):(i + 1) * (S // 2)].rearrange("p (k d) -> p k d", d=128),
                          reads=[vscrb[h // HGc][a_] for a_ in range(i * c.NA // 2, (i + 1) * c.NA // 2)],
                          writes=[vhf[i]])

                def emit_q(h):
                    hl = h % 2
                    pi = h % 2
                    if hl == 0:
                        qs_state["slab"] = load_slab(w_qb, 0, QC, (h // 2) * 768, 768)
                    qslab, qsb = qs_state["slab"]
                    cb0 = hl * 384
                    bn = bank_get()
                    mm_group(bn, bk(bn), [(qslab[:, cc_, cb0:cb0 + 128], qnT[:, cc_, :]) for cc_ in range(QC)],
                             reads=[qsb] + qnb)
                    k.op(DVE, _I('scalar_tensor_tensor', out=qhn[pi], in0=bk(bn), scalar=c.SCALE,
                                 in1=rstdq, op0=ALU.mult, op1=ALU.mult),
                         reads=[bkb(bn), rqb], writes=[qhnb[pi]])
                    bank_put(bn)
                    b1 = bank_get()
                    mm_group(b1, bk(b1), [(qslab[:, cc_, cb0 + 128:cb0 + 256], qnT[:, cc_, :]) for cc_ in range(QC)],
                             reads=[qsb] + qnb)
                    b2 = bank_get()
                    mm_group(b2, bk(b2), [(qslab[:, cc_, cb0 + 256:cb0 + 384], qnT[:, cc_, :]) for cc_ in range(QC)],
                             reads=[qsb] + qnb)
                    k.op(DVE, _I('tensor_tensor', out=rtap[0], in0=bk(b1)[0:64, :], in1=cct, op=ALU.mult),
                         reads=[bkb(b1), ccb], writes=[rtb[0]])
                    k.op(DVE, _I('tensor_tensor', out=rtap[1], in0=bk(b2)[0:64, :], in1=sst, op=ALU.mult),
                         reads=[bkb(b2), ssb], writes=[rtb[1]])
                    bank_put(b1)
                    bank_put(b2)
                    k.op(DVE, _I('tensor_tensor', out=rtap[2], in0=rtap[0], in1=rtap[1], op=ALU.add),
                         reads=[rtb[0], rtb[1]], writes=[rtb[2]])
                    k.op(DVE, _I('scalar_tensor_tensor', out=qhr[pi][0:64, :], in0=rtap[2], scalar=c.SCALE,
                                 in1=rstdq[0:64, :], op0=ALU.mult, op1=ALU.mult),
                         reads=[rtb[2], rqb], pwrites=[qhrb[pi]])

                emit_kv(0, 0)
                emit_kv(0, 1)
                emit_q(0)
                for hp in range(H // 2):
                    for hl in range(2):
                        h = hp * 2 + hl
                        pi = h % 2
                        if h + 1 < H:
                            emit_q(h + 1)
                        bo = bank_get()
                        LA = 2
                        pidx_of = {}
                        for kt in range(NKT + LA):
                            if kt < NKT:
                                bs = bank_get()
                                k.op(PE, _I('matmul', bk(bs), lhsT=KhT[:, kt * 128:(kt + 1) * 128],
                                            rhs=qhn[pi], start=True, stop=False),
                                     reads=[khf[kt // (NKT // 2)], qhnb[pi]], writes=[bkb(bs)], inc=False)
                                k.op(PE, _I('matmul', bk(bs), lhsT=krT_t[:, kt * 128:(kt + 1) * 128],
                                            rhs=qhr[pi], start=False, stop=True),
                                     reads=[krb[kt // 4], qhrb[pi]], pwrites=[bkb(bs)], inc=True)
                                pidx = ptc % 4
                                ptc += 1
                                pidx_of[kt] = pidx
                                k.op(ACT, _I('activation', out=ptap[pidx], in_=bk(bs), func=AF.Exp),
                                     reads=[bkb(bs)], writes=[ptb[pidx]])
                                bank_put(bs)
                                ai = kt % 2
                                if kt < 2:
                                    k.op(DVE, _I('tensor_copy', out=accap[ai], in_=ptap[pidx]),
                                         reads=[ptb[pidx]], writes=[accb[ai]])
                                else:
                                    k.op(DVE, _I('tensor_tensor', out=accap[ai], in0=accap[ai], in1=ptap[pidx],
                                                 op=ALU.add), reads=[ptb[pidx]], writes=[accb[ai]])
                            if kt >= LA:
                                kv_ = kt - LA
                                pidx = pidx_of[kv_]
                                k.op(PE, _I('matmul', bk(bo), lhsT=Vh[:, kv_, :], rhs=ptap[pidx],
                                            start=(kv_ == 0), stop=(kv_ == NKT - 1)),
                                     reads=[vhf[kv_ // (NKT // 2)], ptb[pidx]], writes=[bkb(bo)] if kv_ == 0 else (),
                                     pwrites=() if kv_ == 0 else [bkb(bo)], inc=True)
                                if kv_ == NKT // 2 - 1 and h + 1 < H:
                                    emit_kv(h + 1, 0)
                        if h + 1 < H:
                            emit_kv(h + 1, 1)
                        k.op(DVE, _I('tensor_tensor', out=accap[0], in0=accap[0], in1=accap[1], op=ALU.add),
                             reads=[accb[1]], writes=[accb[0]])
                        k.op(DVE, _I('tensor_copy', out=ahi, in_=accap[0]), reads=[accb[0]], writes=[ahib])
                        k.op(DVE, _I('tensor_tensor', out=alo, in0=accap[0], in1=ahi, op=ALU.subtract),
                             reads=[accb[0], ahib], writes=[alob])
                        bl = bank_get()
                        k.op(PE, _I('matmul', bk(bl), lhsT=ones_t[:], rhs=ahi, start=True, stop=False),
                             reads=[onesb, ahib], writes=[bkb(bl)], inc=False)
                        k.op(PE, _I('matmul', bk(bl), lhsT=ones_t[:], rhs=alo, start=False, stop=True),
                             reads=[alob], pwrites=[bkb(bl)], inc=True)
                        k.op(DVE, _I('reciprocal', out=rl, in_=bk(bl)), reads=[bkb(bl)], writes=[rlb])
                        k.op(DVE, _I('tensor_tensor', out=ybT[:, h, :], in0=bk(bo), in1=rl, op=ALU.mult),
                             reads=[bkb(bo), rlb], writes=[ybb[h]])
                        bank_put(bo)
                        bank_put(bl)

                _ckpt('P2')
                mgb0 = arena.buf(OFF_MG, KC * 1024, "mergedT")
                mgb = [mgb0] + [Buf("mg") for _ in range(KC - 1)]
                inherit(mgb, mgb0)
                mgT = arena.ap(OFF_MG, KC * 1024, BF16, "p (k n) -> p k n", k=KC)
                gtb = [arena.buf(P3_G + i * 2048, 2048, "gt") for i in range(8)]
                gtap = [arena.ap(P3_G + i * 2048, 2048, F32) for i in range(8)]
                gi = 0
                for blk in range(D // 256):
                    gts = []
                    for br in range(2):
                        slab, sbuf_ = load_slab(w_gt, 0, KC, br * D + blk * 256, 256)
                        for j in range(2):
                            ch = blk * 2 + j
                            bi = bank_get()
                            mm_group(bi, bk(bi), [(slab[:, kk, j * 128:(j + 1) * 128], hT[:, kk, 0:512]) for kk in range(KC)],
                                     reads=[sbuf_] + hb)
                            g_ = gi % 8
                            gi += 1
                            k.op(ACT, _I('activation',
                                out=gtap[g_], in_=bk(bi), func=AF.Sigmoid, bias=col(c.c_bgate + br * KC + ch)),
                                reads=[bkb(bi), cbuf], writes=[gtb[g_]])
                            bank_put(bi)
                            gts.append(g_)
                    for br in range(2):
                        slab, sbuf_ = load_slab(w_br, br * CONV, CC, blk * 256, 256)
                        src_bufs = yab if br == 0 else ybb
                        srcT = yaT if br == 0 else ybT
                        for j in range(2):
                            ch = blk * 2 + j
                            g_ = gts[br * 2 + j]
                            bi = bank_get()
                            mm_group(bi, bk(bi), [(slab[:, kk, j * 128:(j + 1) * 128], srcT[:, kk, :]) for kk in range(CC)],
                                     reads=[sbuf_] + src_bufs)
                            if br == 0:
                                k.op(DVE, _I('tensor_tensor', out=gtap[g_], in0=bk(bi), in1=gtap[g_],
                                                                                   op=ALU.mult),
                                     reads=[bkb(bi)], writes=[gtb[g_]])
                            else:
                                ga = gts[j]
                                k.op(DVE, _I('tensor_tensor', out=gtap[g_], in0=bk(bi), in1=gtap[g_],
                                                                                   op=ALU.mult),
                                     reads=[bkb(bi)], writes=[gtb[g_]])
                                k.op(DVE, _I('tensor_tensor', out=mgT[:, ch, :], in0=gtap[g_],
                                                                                          in1=gtap[ga], op=ALU.add),
                                     reads=[gtb[g_], gtb[ga]], writes=[mgb[ch]])
                            bank_put(bi)

                _ckpt('P3')
                retire(hb, hb0); retire(yab, yab0); retire(ybb, ybb0); retire(qnb, qnb0)
                x1b0 = arena.buf(0, 4 * D * 4, "x1")
                x1b = [[Buf("x1") for _ in range(D // 512)] for _ in range(4)]
                for rt in range(4):
                    for b in x1b[rt]:
                        b.r = dict(x1b0.r)
                x1 = [arena.ap(rt * D * 4, D * 4, F32) for rt in range(4)]
                for cb in range(D // 512):
                    bis = [bank_get() for _ in range(4)]
                    nsl = (KC + 15) // 16
                    for s_ in range(nsl):
                        kc_ = min(16, KC - s_ * 16)
                        slab, sbuf_ = load_slab(w_out, s_ * 16 * 128, kc_, cb * 512, 512)
                        for rt in range(4):
                            for kk in range(kc_):
                                kg = s_ * 16 + kk
                                first = (kg == 0)
                                last = (kg == KC - 1)
                                k.op(PE, _I('matmul', bk(bis[rt]), lhsT=mgT[:, kg, rt * 128:(rt + 1) * 128], rhs=slab[:, kk, :],
                                              start=first, stop=last),
                                     reads=([sbuf_] + mgb) if kk == 0 else (),
                                     writes=[bkb(bis[rt])] if first else (), pwrites=() if first else [bkb(bis[rt])],
                                     inc=(kk == kc_ - 1))
                    for rt in range(4):
                        xi = xrcnt[0] % 4
                        xrcnt[0] += 1
                        xrb = arena.buf(P4_XR + xi * 2048, 2048, "xr")
                        xr = arena.ap(P4_XR + xi * 2048, 2048, F32)
                        k.dma(SP, ds_xr[xi], xr, x_own[tb0 + rt * 128:tb0 + (rt + 1) * 128, cb * 512:(cb + 1) * 512],
                              writes=[xrb])
                        k.op(DVE, _I('tensor_tensor',
                            out=x1[rt][:, cb * 512:(cb + 1) * 512], in0=bk(bis[rt]), in1=xr, op=ALU.add),
                            reads=[bkb(bis[rt]), xrb], writes=[x1b[rt][cb]])
                        bank_put(bis[rt])

                _ckpt('P4')
                retire(mgb, mgb0)
                h2b0 = arena.buf(OFF_MG, KC * 1024, "h2T")
                h2b = [h2b0] + [Buf("h2") for _ in range(KC - 1)]
                inherit(h2b, h2b0)
                h2T = arena.ap(OFF_MG, KC * 1024, BF16, "p (k n) -> p k n", k=KC)
                def ffn_stage1(rt):
                    xo_ = P5_XN if rt % 2 == 0 else P5_ACT
                    xnb = arena.buf(xo_, D * 2, "xn2")
                    xn = arena.ap(xo_, D * 2, BF16)
                    si = 12 + 4 * (rt % 2)
                    ssq = stat_t[:, si:si + 1]
                    std = stat_t[:, si + 1:si + 2]
                    rstd = stat_t[:, si + 2:si + 3]
                    sb_ = statb[si]
                    k.op(ACT, _I('activation', out=xn, in_=x1[rt], func=AF.Square, accum_out=ssq),
                         reads=x1b[rt], writes=[xnb, sb_])
                    k.op(ACT, _I('activation', out=std, in_=ssq, func=AF.Sqrt, bias=col(c.c_eps),
                                 scale=1.0 / D), reads=[sb_, cbuf], pwrites=[sb_])
                    k.op(DVE, _I('reciprocal', out=rstd, in_=std), reads=[sb_], pwrites=[sb_])
                    k.op(DVE, _I('tensor_scalar', out=xn, in0=x1[rt], scalar1=rstd, scalar2=None, op0=ALU.mult),
                         reads=[sb_], writes=[xnb])
                    return (xn, xnb)

                f1 = {0: ffn_stage1(0)}
                for rt in range(4):
                    if rt + 1 < 4:
                        f1[rt + 1] = ffn_stage1(rt + 1)
                    rows_stage2(f1[rt], h2T, h2b, rt * 128, 128, c.c_gffn)
                sgb = [arena.buf(P5_XN + i * 2048, 2048, "sg") for i in range(4)]
                sgap = [arena.ap(P5_XN + i * 2048, 2048, F32) for i in range(4)]
                sgi = 0
                f0 = 0
                while f0 < FC:
                    G = min(FG, FC - f0)
                    acb0 = arena.buf(P5_ACT, FG * 1024, "actT")
                    acb = [acb0] + [Buf("ac") for _ in range(G - 1)]
                    inherit(acb, acb0)
                    acT = arena.ap(P5_ACT, FG * 1024, BF16, "p (k n) -> p k n", k=FG)
                    for blk in range(G // 2):
                        cbase = (f0 + blk * 2) * 128
                        gslab, gsb = load_slab(w_g, 0, KC, cbase, 256)
                        uslab, usb = load_slab(w_u, 0, KC, cbase, 256)
                        for j in range(2):
                            fl = blk * 2 + j
                            bg = bank_get()
                            mm_group(bg, bk(bg), [(gslab[:, kk, j * 128:(j + 1) * 128], h2T[:, kk, :]) for kk in range(KC)],
                                     reads=[gsb] + h2b)
                            bu = bank_get()
                            mm_group(bu, bk(bu), [(uslab[:, kk, j * 128:(j + 1) * 128], h2T[:, kk, :]) for kk in range(KC)],
                                     reads=[usb] + h2b)
                            s_ = sgi % 4
                            sgi += 1
                            k.op(ACT, _I('activation', out=sgap[s_], in_=bk(bg), func=AF.Silu),
                                 reads=[bkb(bg)], writes=[sgb[s_]])
                            bank_put(bg)
                            k.op(DVE, _I('tensor_tensor', out=acT[:, fl, :], in0=bk(bu),
                                                                                               in1=sgap[s_], op=ALU.mult),
                                 reads=[bkb(bu), sgb[s_]], writes=[acb[fl]])
                            bank_put(bu)
                    for cp in range(D // 1024):
                        dslab, dsb = load_slab(w_d, f0 * 128, G, cp * 1024, 1024)
                        for half in range(2):
                            cb = cp * 2 + half
                            for rt in range(4):
                                bi = bank_get()
                                mm_group(bi, bk(bi), [(acT[:, fl, rt * 128:(rt + 1) * 128],
                                                       dslab[:, fl, half * 512:(half + 1) * 512]) for fl in range(G)],
                                         reads=[dsb] + acb)
                                k.op(DVE, _I('tensor_tensor',
                                    out=x1[rt][:, cb * 512:(cb + 1) * 512], in0=bk(bi),
                                    in1=x1[rt][:, cb * 512:(cb + 1) * 512], op=ALU.add),
                                    reads=[bkb(bi)], writes=[x1b[rt][cb]])
                                bank_put(bi)
                    retire(acb, acb0)
                    f0 += G

                _ckpt('P5')
                retire(h2b, h2b0)
                gfb = arena.buf(PF_G, D * 4, "gfin")
                gf = arena.ap(PF_G, D * 4, F32)
                k.dma(SP, ds_gf, gf, gfin_d, writes=[gfb])
                for rt in range(4):
                    jb = arena.buf(PF_J, D * 2, "junk")
                    junk = arena.ap(PF_J, D * 2, BF16)
                    si = 20 + 4 * (rt % 2)
                    ssq = stat_t[:, si:si + 1]
                    std = stat_t[:, si + 1:si + 2]
                    rstd = stat_t[:, si + 2:si + 3]
                    sb_ = statb[si]
                    k.op(ACT, _I('activation', out=junk, in_=x1[rt], func=AF.Square,
                                                                               accum_out=ssq),
                         reads=x1b[rt], writes=[jb, sb_])
                    k.op(ACT, _I('activation', out=std, in_=ssq, func=AF.Sqrt, bias=col(c.c_eps),
                                                                       scale=1.0 / D), reads=[sb_, cbuf], pwrites=[sb_])
                    k.op(DVE, _I('reciprocal', out=rstd, in_=std), reads=[sb_], pwrites=[sb_])
                    k.op(DVE, _I('scalar_tensor_tensor', out=x1[rt], in0=x1[rt], scalar=rstd, in1=gf,
                                                                                  op0=ALU.mult, op1=ALU.mult),
                         reads=[sb_, gfb], writes=x1b[rt])
                    r0 = tj * 512 + rt * 128
                    k.dma(SP, ds_st[rt], out_d[r0:r0 + 128, :], x1[rt], reads=x1b[rt])
                for rt in range(4):
                    for b in x1b[rt]:
                        _merge(x1b0.r, b.r)
                        _merge(x1b0.w, b.w)
        except _Stop:
            pass

        for d in all_dsems:
            if d.cnt:
                SP.stream.append(("w", d.sem, d.cnt))
        for en in (PE, ACT, DVE, POOL):
            if en.cnt:
                SP.stream.append(("w", en.sem, en.cnt))

        with nc.Block() as block:
            @block.tensor
            def _(e):
                _replay(e, PE)

            @block.scalar
            def _(e):
                _replay(e, ACT)

            @block.vector
            def _(e):
                _replay(e, DVE)

            @block.gpsimd
            def _(e):
                _replay(e, POOL)

            @block.sync
            def _(e):
                _replay(e, SP)
    return nc


def slab_geometry(c):
    HG = min(4, c.H)
    return {
        "w_a": (c.KC, 256), "w_kvin": (c.KC, 256), "w_gt": (c.KC, 256), "w_qb": (c.QC, 768),
        "w_kvb": (c.KVC, HG * 128), "w_br": (c.CC, 256), "w_out": (min(16, c.KC), 512),
        "w_g": (c.KC, 256), "w_u": (c.KC, 256), "w_d": (8, 1024),
    }


def slabify(W, kcf, ncols):
    W = np.asarray(W, np.float32)
    R, N = W.shape
    blk = kcf * 128
    nr = (R + blk - 1) // blk
    if nr * blk != R:
        Wp = np.zeros((nr * blk, N), np.float32)
        Wp[:R] = W
        W = Wp
    assert N % ncols == 0
    ncb = N // ncols
    out = W.reshape(nr, kcf, 128, ncb, ncols).transpose(0, 3, 2, 1, 4)
    return np.ascontiguousarray(out).reshape(nr, ncb, 128, kcf * ncols)


def host_constants(cfg):
    invf = (ROPE_THETA ** (-np.arange(0, 64, 2, dtype=np.float32) / np.float32(64))).astype(np.float32)
    return invf


def prep_shared(cfg, inp):
    c = cfg
    D, CONV, QL, KVL, H = c.D, c.CONV, c.QL, c.KVL, c.H
    w_in = np.asarray(inp["w_in"])[0]
    o_q = 3 * CONV
    o_kv = o_q + QL
    o_kr = o_kv + KVL
    o_g = o_kr + 64
    kr = w_in[:, o_kr:o_kr + 64]
    kr_sw = np.concatenate([kr[:, 32:], kr[:, :32]], axis=1)
    sh = {}
    sh["w_a"] = np.ascontiguousarray(w_in[:, :o_kv])
    sh["w_kvin"] = np.ascontiguousarray(np.concatenate([w_in[:, o_kv:o_kr], kr, kr_sw, kr_sw, kr], axis=1))
    sh["w_gt"] = np.ascontiguousarray(w_in[:, o_g:o_g + 2 * D])
    wq = np.asarray(inp["w_q_b"])[0].reshape(QL, H, 192)
    qn, qr = wq[:, :, :128], wq[:, :, 128:]
    qr_sw = np.concatenate([qr[:, :, 32:], qr[:, :, :32]], axis=2)
    sh["w_qb"] = np.ascontiguousarray(np.concatenate([qn, qr, qr_sw, qr_sw, qr], axis=2).reshape(QL, H * 384))
    wkv = np.asarray(inp["w_kv_b"])[0].reshape(KVL, H, 256)
    kk = wkv[:, :, :128].reshape(KVL, H * 128)
    vv = wkv[:, :, 128:].reshape(KVL, H * 128)
    sh["w_kvb"] = np.ascontiguousarray(np.concatenate([kk, vv], axis=1))
    sh["w_br"] = np.ascontiguousarray(np.asarray(inp["w_branch"])[0].reshape(2 * CONV, D))
    sh["w_out"] = np.ascontiguousarray(np.asarray(inp["w_out"])[0])
    sh["w_g"] = np.ascontiguousarray(np.asarray(inp["w_ffn_gate"])[0])
    sh["w_u"] = np.ascontiguousarray(np.asarray(inp["w_ffn_up"])[0])
    sh["w_d"] = np.ascontiguousarray(np.asarray(inp["w_ffn_down"])[0])

    geo = slab_geometry(c)
    for name in list(sh.keys()):
        sh[name] = slabify(sh[name], *geo[name])

    def colz(v):
        v = np.asarray(v, dtype=np.float32).reshape(-1, 128)
        return v.T
    cols = np.zeros((128, c.NCOLS), np.float32)
    cols[:, c.c_gmix:c.c_gmix + c.KC] = colz(np.asarray(inp["g_mix"])[0])
    cols[:, c.c_bgate:c.c_bgate + 2 * c.KC] = colz(np.asarray(inp["b_gate"])[0])
    cw = np.asarray(inp["conv_w"])[0]
    for tap in range(3):
        cols[:, c.c_convw + tap * c.CC:c.c_convw + (tap + 1) * c.CC] = colz(cw[tap])
    cols[:, c.c_gqa:c.c_gqa + c.QC] = colz(np.asarray(inp["g_q_a"])[0])
    cols[:, c.c_gkva:c.c_gkva + c.KVC] = colz(np.asarray(inp["g_kv_a"])[0])
    cols[:, c.c_gffn:c.c_gffn + c.KC] = colz(np.asarray(inp["g_ffn"])[0])
    invf = host_constants(c)
    cols[0:32, c.c_invf] = invf
    cols[32:64, c.c_invf] = invf
    cols[0:32, c.c_sign] = -1.0
    cols[32:64, c.c_sign] = 1.0
    cols[:, c.c_eps] = RMS_EPS
    sh["cols"] = cols
    sh["gfin"] = np.ascontiguousarray(np.broadcast_to(np.asarray(inp["g_final"], np.float32)[None, :], (128, D)))
    sh["ident"] = np.eye(128, dtype=np.float32)
    return sh


def prep_core(cfg, x, positions, b, half):
    c = cfg
    S, D, NOWN = c.S, c.D, c.NOWN
    xs = np.ascontiguousarray(x[b])
    start = half * NOWN
    tiles = np.zeros((c.NT, 514, D), np.float32)
    for j in range(c.NT):
        t0 = start + j * 512
        tiles[j, :512] = xs[t0:t0 + 512]
        if t0 - 1 >= 0:
            tiles[j, 512] = xs[t0 - 1]
        if t0 + 512 < S:
            tiles[j, 513] = xs[t0 + 512]
    pos = np.asarray(positions[b], np.int32)
    m = {
        "x_seq": xs,
        "x_own": tiles.reshape(c.NT * 514, D),
        "pos_seq": np.ascontiguousarray(np.broadcast_to(pos[None, :], (64, S))),
        "pos_own": np.ascontiguousarray(np.broadcast_to(pos[None, start:start + NOWN], (64, NOWN))),
    }
    return m


_NC_CACHE = {}


def kernel(x, positions, g_mix, w_in, b_gate, conv_w, g_q_a, w_q_b, g_kv_a, w_kv_b,
           w_branch, w_out, g_ffn, w_ffn_gate, w_ffn_up, w_ffn_down, g_final):
    cfg = Cfg()
    inp = dict(g_mix=g_mix, w_in=w_in, b_gate=b_gate, conv_w=conv_w, g_q_a=g_q_a, w_q_b=w_q_b,
               g_kv_a=g_kv_a, w_kv_b=w_kv_b, w_branch=w_branch, w_out=w_out, g_ffn=g_ffn,
               w_ffn_gate=w_ffn_gate, w_ffn_up=w_ffn_up, w_ffn_down=w_ffn_down, g_final=g_final)
    x = np.asarray(x, np.float32)
    positions = np.asarray(positions)
    B = x.shape[0]
    sh = prep_shared(cfg, inp)
    in_maps = []
    for core in range(8):
        b, half = core // 2, core % 2
        m = prep_core(cfg, x, positions, b, half)
        m.update(sh)
        in_maps.append(m)
    if "nc" not in _NC_CACHE:
        _NC_CACHE["nc"] = build(cfg)
    nc = _NC_CACHE["nc"]
    res = run_bass_kernel_spmd(nc, in_maps, core_ids=list(range(8)))
    out = np.zeros((B, cfg.S, cfg.D), np.float32)
    for core in range(8):
        b, half = core // 2, core % 2
        out[b, half * cfg.NOWN:(half + 1) * cfg.NOWN] = np.asarray(res.results[core]["out"], np.float32)
    return out
```

```python
import math
import sys
from contextlib import ExitStack

import numpy as np
import concourse.bass as bass
import concourse.mybir as mybir
from concourse.bass_utils import run_bass_kernel_spmd

F32 = mybir.dt.float32
BF16 = mybir.dt.bfloat16
I32 = mybir.dt.int32
AF = mybir.ActivationFunctionType
ALU = mybir.AluOpType
AX = mybir.AxisListType

RMS_EPS = 1e-6
ROPE_THETA = 10000.0
TWO_PI = 2.0 * math.pi
PI_SAFE = 3.1415925


DEBUG_LINES = False
LINE_MAP = {}


class Cfg:
    def __init__(s, D=4096, S=4096, NOWN=2048, CONV=2048, H=16, QL=1024, KVL=512, DFF=11008):
        s.D, s.S, s.NOWN, s.CONV, s.H, s.QL, s.KVL, s.DFF = D, S, NOWN, CONV, H, QL, KVL, DFF
        s.KC = D // 128
        s.NT = NOWN // 512
        s.NA = S // 512
        s.CC = CONV // 128
        s.QC = QL // 128
        s.KVC = KVL // 128
        s.FC = DFF // 128
        s.SCALE = 1.0 / math.sqrt(192.0)
        o = 0
        s.c_gmix = o; o += s.KC
        s.c_bgate = o; o += 2 * s.KC
        s.c_convw = o; o += 3 * s.CC
        s.c_gqa = o; o += s.QC
        s.c_gkva = o; o += s.KVC
        s.c_gffn = o; o += s.KC
        s.c_invf = o; o += 1
        s.c_sign = o; o += 1
        s.c_eps = o; o += 1
        s.c_zero = o; o += 1
        s.NCOLS = o


class _Stop(Exception):
    pass


STOP_AFTER = None
EVAC_MODE = 0


def _ckpt(name):
    if STOP_AFTER == name:
        raise _Stop()


class Buf:
    __slots__ = ("w", "r", "war", "name", "excl")

    def __init__(self, name="", excl=False):
        self.w = {}
        self.r = {}
        self.war = {}
        self.name = name
        self.excl = excl


def _merge(d, src):
    for k, (sem, v) in src.items():
        cur = d.get(k)
        if cur is None or cur[1] < v:
            d[k] = (sem, v)


class Eng:
    def __init__(self, name, sem):
        self.name = name
        self.sem = sem
        self.cnt = 0
        self.seen = {}
        self.stream = []


class DSem:
    def __init__(self, key, sem):
        self.key = key
        self.sem = sem
        self.cnt = 0


class Arena:
    def __init__(self, tensor, nbytes):
        self.t = tensor
        self.nbytes = nbytes
        self.live = []

    def buf(self, lo, nbytes, name=""):
        hi = lo + nbytes
        assert hi <= self.nbytes, (name, lo, nbytes, self.nbytes)
        b = Buf(name)
        keep = []
        for (l, h, ob) in self.live:
            if l < hi and lo < h:
                _merge(b.r, ob.w)
                _merge(b.r, ob.r)
                _merge(b.r, ob.war)
                if not (lo <= l and h <= hi):
                    keep.append((l, h, ob))
            else:
                keep.append((l, h, ob))
        keep.append((lo, hi, b))
        self.live = keep
        return b

    def ap(self, lo, nbytes, dtype, pattern=None, **kw):
        assert lo % 4 == 0 and nbytes % 4 == 0
        a = self.t[:, lo // 4:(lo + nbytes) // 4]
        if dtype != F32:
            a = a.bitcast(dtype)
        if pattern is not None:
            a = a.rearrange(pattern, **kw)
        return a


class K:
    def __init__(self, nc, cfg):
        self.nc = nc
        self.cfg = cfg
        self.engs = {}
        self.nsem = 0

    def new_eng(self, name, sem):
        e = Eng(name, sem)
        self.engs[name] = e
        return e

    def _wait(self, eng, deps):
        for key, (sem, val) in deps.items():
            if eng.name == "pe" and key == "pe":
                continue
            if eng.seen.get(key, 0) < val:
                eng.stream.append(("w", sem, val))
                eng.seen[key] = val

    def op(self, eng, fn, reads=(), writes=(), pwrites=(), inc=True):
        deps = {}
        for b in reads:
            _merge(deps, b.w)
            if b.excl:
                _merge(deps, {kk_: vv_ for kk_, vv_ in b.r.items() if kk_ != eng.name})
        for b in writes:
            _merge(deps, b.w)
            _merge(deps, b.r)
        for b in pwrites:
            _merge(deps, b.r)
            _merge(deps, b.war)
        self._wait(eng, deps)
        if inc:
            eng.cnt += 1
            val = eng.cnt
        else:
            val = eng.cnt + 1
        eng.stream.append(("i", fn, inc, sys._getframe(1).f_lineno if DEBUG_LINES else 0))
        tok = {eng.name: (eng.sem, val)}
        for b in reads:
            _merge(b.r, tok)
        for b in writes:
            nw = dict(b.w)
            _merge(nw, b.r)
            b.war = nw
            b.w = dict(tok)
            b.r = {}
        for b in pwrites:
            _merge(b.w, tok)
        return tok

    def dma(self, q, dsem, out, in_, reads=(), writes=(), pwrites=()):
        deps = {}
        for b in reads:
            _merge(deps, b.w)
        for b in writes:
            _merge(deps, b.w)
            _merge(deps, b.r)
        for b in pwrites:
            _merge(deps, b.r)
            _merge(deps, b.war)
        self._wait(q, deps)
        dsem.cnt += 16
        q.stream.append(("d", out, in_, dsem.sem))
        tok = {dsem.key: (dsem.sem, dsem.cnt)}
        for b in reads:
            _merge(b.r, tok)
        for b in writes:
            nw = dict(b.w)
            _merge(nw, b.r)
            b.war = nw
            b.w = dict(tok)
            b.r = {}
        for b in pwrites:
            _merge(b.w, tok)
        return tok


def _I(name, *args, **kwargs):
    return (name, args, kwargs)


def _replay(e, eng):
    for item in eng.stream:
        kind = item[0]
        if kind == "w":
            e.wait_ge(item[1], item[2])
        elif kind == "i":
            name, args, kwargs = item[1]
            ins = getattr(e, name)(*args, **kwargs)
            if DEBUG_LINES:
                try:
                    LINE_MAP[str(ins.ins.name)] = item[3]
                except Exception:
                    pass
            if item[2]:
                ins.then_inc(eng.sem, 1)
        else:
            e.dma_start(out=item[1], in_=item[2]).then_inc(item[3], 16)


def build(cfg):
    nc = bass.Bass("TRN2", target_bir_lowering=False)
    c = cfg
    D, S, KC, CC, QC, KVC, FC, H = c.D, c.S, c.KC, c.CC, c.QC, c.KVC, c.FC, c.H
    CONV, QL, KVL, DFF = c.CONV, c.QL, c.KVL, c.DFF
    NKT = S // 128
    NKB = S // 512

    def din(name, shape, dt=F32):
        return nc.dram_tensor(name, list(shape), dt, kind="ExternalInput").ap()

    x_seq = din("x_seq", [S, D])
    x_own = din("x_own", [c.NT * 514, D])
    pos_seq = din("pos_seq", [64, S], I32)
    pos_own = din("pos_own", [64, c.NOWN], I32)
    class WS:
        def __init__(self, name, rows, cols):
            kcf, ncols = slab_geometry(c)[name]
            self.kcf, self.ncols = kcf, ncols
            nr = (rows + kcf * 128 - 1) // (kcf * 128)
            assert cols % ncols == 0
            self.ap = din(name, [nr, cols // ncols, 128, kcf * ncols])

    w_a = WS("w_a", D, 3 * CONV + QL)
    w_kvin = WS("w_kvin", D, KVL + 256)
    w_gt = WS("w_gt", D, 2 * D)
    w_qb = WS("w_qb", QL, H * 384)
    w_kvb = WS("w_kvb", KVL, H * 256)
    w_br = WS("w_br", 2 * CONV, D)
    w_out = WS("w_out", D, D)
    w_g = WS("w_g", D, DFF)
    w_u = WS("w_u", D, DFF)
    w_d = WS("w_d", DFF, D)
    cols_d = din("cols", [128, c.NCOLS])
    gfin_d = din("gfin", [128, D])
    ident_d = din("ident", [128, 128])
    out_d = nc.dram_tensor("out", [c.NOWN, D], F32, kind="ExternalOutput").ap()
    kscr = nc.dram_tensor("kscr", [H, 128, S], BF16, kind="Internal").ap()
    vscr = nc.dram_tensor("vscr", [H, 128, S], BF16, kind="Internal").ap()

    def al(v, a=256):
        return (v + a - 1) // a * a
    HT_SZ = KC * 514 * 2
    OFF1 = al(HT_SZ)
    XST = D * 4
    YB_SZ = CC * 1024
    KH_SZ = S * 2
    OFF_YB = OFF1
    OFF_KH = OFF_YB + YB_SZ
    OFF_VH = OFF_KH + KH_SZ
    OFF2 = max(OFF1 + 2 * XST, OFF_VH + KH_SZ, 4 * D * 4)
    YA_SZ = max(CC * 1024, 16384)
    OFF3 = OFF2 + YA_SZ
    QN_SZ = QC * 1024
    OFF_RQ = OFF3 + QN_SZ
    OFF4 = OFF_RQ + 2048
    TMP_SZ = 26624
    ARENA = max(OFF4 + TMP_SZ, OFF3 + KC * 1024, OFF2 + 51200)
    if 2 * KH_SZ >= 16384:
        P3_G_OFF = OFF_KH
    else:
        P3_G_OFF = ARENA
        ARENA += 16384
    SLOT = 16384
    NSLOT = 3

    es = ExitStack()
    with es:
        def sb(name, shape, dt):
            return es.enter_context(nc.sbuf_tensor(name, list(shape), dt))

        arena_t = sb("arena", [128, ARENA // 4], F32)
        ring_t = sb("ring", [128, NSLOT * SLOT // 2], BF16)
        kvnT_t = sb("kvnT", [128, KVC, S], BF16)
        krT_t = sb("krT", [128, S], BF16)
        cols_t = sb("colsb", [128, c.NCOLS], F32)
        ident_t = sb("identb", [128, 128], BF16)
        ones_t = sb("onesb", [128, 128], BF16)
        stat_t = sb("stat", [128, 64], F32)
        psum_t = [es.enter_context(nc.psum_tensor("ps%d" % i, [128, 512], F32)) for i in range(8)]

        def sem(name):
            return es.enter_context(nc.semaphore(name))

        k = K(nc, c)
        PE = k.new_eng("pe", sem("s_pe"))
        ACT = k.new_eng("act", sem("s_act"))
        DVE = k.new_eng("dve", sem("s_dve"))
        POOL = k.new_eng("pool", sem("s_pool"))
        SP = k.new_eng("sp", sem("s_sp"))

        all_dsems = []

        def dsem(name):
            d = DSem(name, sem(name))
            all_dsems.append(d)
            return d

        arena = Arena(arena_t, ARENA)

        def col(off, n=1, p=128):
            return cols_t[0:p, off:off + n]

        banks = [(psum_t[i], Buf("bank%d" % i, excl=True)) for i in range(8)]
        free_banks = list(range(8))

        def bank_get():
            return free_banks.pop(0)

        def bank_put(i):
            free_banks.append(i)

        def bk(i):
            return psum_t[i][:]

        def bkb(i):
            return banks[i][1]

        cbuf = Buf("consts")
        ds_const = dsem("d_const")
        k.dma(SP, ds_const, cols_t[:], cols_d, writes=[cbuf])
        ds_const2 = dsem("d_const2")
        k.dma(POOL, ds_const2, ident_t[:], ident_d, pwrites=[cbuf])
        onesb = Buf("ones")
        k.op(DVE, _I('memset', ones_t[:], 1.0), writes=[onesb])
        krbuf_z = Buf("krz")
        k.op(DVE, _I('memset', krT_t[:], 0.0), writes=[krbuf_z])
        statb = [Buf("stat%d" % i) for i in range(64)]

        ring = []
        for i in range(NSLOT):
            ring.append((Buf("slot%d" % i), dsem("d_slot%d" % i)))
        ring_pos = [0]

        def load_slab(w_ap, r0, kc, c0, ncols):
            assert kc * ncols * 2 <= SLOT
            i = ring_pos[0] % NSLOT
            ring_pos[0] += 1
            b, ds = ring[i]
            base = i * (SLOT // 2)
            ap = ring_t[:, base:base + kc * ncols].rearrange("p (k n) -> p k n", k=kc)
            ws = w_ap
            assert r0 % (ws.kcf * 128) == 0 and c0 % ws.ncols == 0 and kc <= ws.kcf and ncols == ws.ncols, \
                (r0, kc, c0, ncols, ws.kcf, ws.ncols)
            src = ws.ap[r0 // (ws.kcf * 128), c0 // ws.ncols][:, 0:kc * ncols].rearrange("p (k n) -> p k n", k=kc)
            k.dma(POOL, ds, ap, src, writes=[b])
            return ap, b

        def mm_group(bank, out_ap, pairs, reads, partial=False):
            n = len(pairs)
            tok = None
            for j, (l, r) in enumerate(pairs):
                last = (j == n - 1)
                tok = k.op(PE, (_I('matmul', out_ap, lhsT=l, rhs=r, start=(j == 0), stop=last)),
                           reads=reads if j == 0 else (),
                           writes=() if (partial or j > 0) else [bkb(bank)],
                           pwrites=[bkb(bank)] if (partial or j > 0) else (),
                           inc=last)
            return tok

        rr = [0]

        def evac_eng():
            rr[0] += 1
            if EVAC_MODE == 1:
                return ACT
            if EVAC_MODE == 2:
                return DVE
            return ACT if rr[0] % 2 else DVE

        def scaled_copy(eng, out, in_, scale_ap, reads, writes=(), pwrites=()):
            if eng is ACT:
                return k.op(ACT, _I('activation', out=out, in_=in_, func=AF.Copy, scale=scale_ap),
                            reads=reads, writes=writes, pwrites=pwrites)
            return k.op(DVE, _I('tensor_scalar', out=out, in0=in_, scalar1=scale_ap, scalar2=None,
                                                        op0=ALU.mult),
                        reads=reads, writes=writes, pwrites=pwrites)

        xst = [(arena_t, OFF1 + i * XST) for i in range(2)]
        ds_x = [dsem("d_x0"), dsem("d_x1")]
        xcnt = [0]

        def x_prefetch(src_rows, nrows):
            i = xcnt[0] % 2
            xcnt[0] += 1
            xb = arena.buf(xst[i][1], XST, "xst")
            xap = arena.ap(xst[i][1], XST, F32)
            k.dma(SP, ds_x[i], xap[0:nrows, :], src_rows, writes=[xb])
            return (i, xb, xap)

        def rows_stage1(pre, xn_offs):
            i, xb, xap = pre
            xn_off = xn_offs[i]
            stat_i = 4 * i
            xnb = arena.buf(xn_off, D * 2, "xn")
            xn = arena.ap(xn_off, D * 2, BF16)
            ssq = stat_t[:, stat_i:stat_i + 1]
            std = stat_t[:, stat_i + 1:stat_i + 2]
            rstd = stat_t[:, stat_i + 2:stat_i + 3]
            sb_ = statb[stat_i]
            k.op(ACT, _I('activation', out=xn, in_=xap, func=AF.Square, accum_out=ssq),
                 reads=[xb], writes=[xnb, sb_])
            k.op(ACT, _I('activation', out=std, in_=ssq, func=AF.Sqrt, bias=col(c.c_eps),
                         scale=1.0 / D), reads=[sb_, cbuf], pwrites=[sb_])
            k.op(DVE, _I('reciprocal', out=rstd, in_=std), reads=[sb_], pwrites=[sb_])
            k.op(DVE, _I('tensor_scalar', out=xn, in0=xap, scalar1=rstd, scalar2=None, op0=ALU.mult),
                 reads=[xb, sb_], writes=[xnb])
            return (xn, xnb)

        def rows_stage2(s1, hT_ap, hT_bufs, col0, ncol, gcol):
            xn, xnb = s1
            for g in range(0, KC, 8):
                bi = bank_get()
                pb = bk(bi).bitcast(BF16)
                ng = min(8, KC - g)
                for j in range(ng):
                    kk = g + j
                    k.op(PE, (_I('transpose', out=pb[:, j * 128:(j + 1) * 128],
                                 in_=xn[:, kk * 128:(kk + 1) * 128], identity=ident_t[:])),
                         reads=[xnb, cbuf] if j == 0 else (),
                         writes=[bkb(bi)] if j == 0 else (), pwrites=() if j == 0 else [bkb(bi)],
                         inc=(j == ng - 1))
                eng_ = evac_eng()
                for j in range(ng):
                    kk = g + j
                    scaled_copy(eng_, hT_ap[:, kk, col0:col0 + ncol], pb[:, j * 128:j * 128 + ncol],
                                col(gcol + kk), reads=[bkb(bi), cbuf], pwrites=[hT_bufs[kk]])
                bank_put(bi)

        ds_pos = dsem("d_pos")

        def trig_tables(pos_ap, off, cc_off, ss_off):
            pb_ = arena.buf(off, 2048, "posi")
            posi = arena.ap(off, 2048, I32)[0:64, :]
            ab = arena.buf(off + 2048, 2048, "ang")
            ang = arena.ap(off + 2048, 2048, F32)[0:64, :]
            tb = arena.buf(off + 4096, 2048, "trt")
            tt = arena.ap(off + 4096, 2048, F32)[0:64, :]
            ib = arena.buf(off + 6144, 2048, "tri")
            ti = arena.ap(off + 6144, 2048, I32)[0:64, :]
            ccb = arena.buf(cc_off, 2048, "cct")
            cct = arena.ap(cc_off, 2048, F32)[0:64, :]
            ssb = arena.buf(ss_off, 2048, "sst")
            sst = arena.ap(ss_off, 2048, F32)[0:64, :]
            k.dma(SP, ds_pos, posi, pos_ap, writes=[pb_])
            V = DVE
            k.op(V, _I('tensor_copy', out=ang, in_=posi), reads=[pb_], writes=[ab])
            k.op(V, _I('tensor_scalar', out=ang, in0=ang, scalar1=col(c.c_invf, 1, 64), scalar2=None,
                                              op0=ALU.mult), reads=[cbuf], writes=[ab])

            def reduce_to(dst, dstb, shift):
                k.op(V, _I('tensor_scalar', out=tt, in0=ang, scalar1=1.0 / TWO_PI,
                                                  scalar2=0.5 + shift / TWO_PI, op0=ALU.mult, op1=ALU.add),
                     reads=[ab], writes=[tb])
                k.op(V, _I('tensor_copy', out=ti, in_=tt), reads=[tb], writes=[ib])
                k.op(V, _I('tensor_copy', out=tt, in_=ti), reads=[ib], writes=[tb])
                C1 = 6.28125
                C2 = TWO_PI - 6.28125
                k.op(V, _I('scalar_tensor_tensor', out=dst, in0=tt, scalar=-C1, in1=ang,
                                                         op0=ALU.mult, op1=ALU.add), reads=[tb, ab], writes=[dstb])
                k.op(V, _I('scalar_tensor_tensor', out=dst, in0=tt, scalar=-C2, in1=dst,
                                                         op0=ALU.mult, op1=ALU.add), reads=[tb], writes=[dstb])
                if shift != 0.0:
                    k.op(V, _I('tensor_scalar', out=dst, in0=dst, scalar1=shift, scalar2=None,
                                                      op0=ALU.add), writes=[dstb])
                k.op(V, _I('tensor_scalar', out=tt, in0=dst, scalar1=-math.pi, scalar2=TWO_PI,
                                                  op0=ALU.is_lt, op1=ALU.mult), reads=[dstb], writes=[tb])
                k.op(V, _I('tensor_tensor', out=dst, in0=dst, in1=tt, op=ALU.add), reads=[tb], writes=[dstb])
                k.op(V, _I('tensor_scalar', out=tt, in0=dst, scalar1=math.pi, scalar2=-TWO_PI,
                                                  op0=ALU.is_gt, op1=ALU.mult), reads=[dstb], writes=[tb])
                k.op(V, _I('tensor_tensor', out=dst, in0=dst, in1=tt, op=ALU.add), reads=[tb], writes=[dstb])
                k.op(V, _I('tensor_scalar', out=dst, in0=dst, scalar1=PI_SAFE, scalar2=-PI_SAFE,
                                                  op0=ALU.min, op1=ALU.max), writes=[dstb])

            reduce_to(sst, ssb, 0.0)
            reduce_to(cct, ccb, math.pi / 2)
            k.op(ACT, _I('activation', out=sst, in_=sst, func=AF.Sin, scale=col(c.c_sign, 1, 64)),
                 reads=[cbuf], writes=[ssb])
            k.op(ACT, _I('activation', out=cct, in_=cct, func=AF.Sin), writes=[ccb])
            return cct, ccb, sst, ssb

        def stat_begin(off, off2=None):
            return {"bank": bank_get(), "n": 0, "off": off, "off2": off2, "pend": None, "nmm": 0}

        def _stat_mm(st, last):
            hi, hib, lo, lob = st["pend"]
            sbk = st["bank"]
            first = (st["nmm"] == 0)
            st["nmm"] += 1
            k.op(PE, _I('matmul', bk(sbk), lhsT=ones_t[:], rhs=hi, start=first, stop=False),
                 reads=[hib, onesb], writes=[bkb(sbk)] if first else (),
                 pwrites=() if first else [bkb(sbk)], inc=False)
            k.op(PE, _I('matmul', bk(sbk), lhsT=ones_t[:], rhs=lo, start=False, stop=last),
                 reads=[lob], pwrites=[bkb(sbk)], inc=True)
            st["pend"] = None

        def stat_add(st, bi, last=False):
            if st["pend"] is not None:
                _stat_mm(st, False)
            off = st["off"]
            par = st["n"] % 2
            st["n"] += 1
            hoff = off + 2048 if (par == 0 or st["off2"] is None) else st["off2"]
            sqb = arena.buf(off, 2048, "sq")
            sq = arena.ap(off, 2048, F32)
            hib = arena.buf(hoff, 1024, "hi")
            hi = arena.ap(hoff, 1024, BF16)
            lob = arena.buf(hoff + 1024, 1024, "lo")
            lo = arena.ap(hoff + 1024, 1024, BF16)
            k.op(ACT, _I('activation', out=sq, in_=bk(bi), func=AF.Square),
                 reads=[bkb(bi)], writes=[sqb])
            k.op(DVE, _I('tensor_copy', out=hi, in_=sq), reads=[sqb], writes=[hib])
            k.op(DVE, _I('tensor_tensor', out=lo, in0=sq, in1=hi, op=ALU.subtract),
                 reads=[sqb, hib], writes=[lob])
            st["pend"] = (hi, hib, lo, lob)

        def stat_finish(st, n_feat, out_ap, outb):
            _stat_mm(st, True)
            sbk = st["bank"]
            k.op(ACT, _I('activation', out=out_ap, in_=bk(sbk), func=AF.Sqrt, bias=col(c.c_eps),
                                             scale=1.0 / n_feat), reads=[bkb(sbk), cbuf], writes=[outb])
            bank_put(sbk)
            k.op(DVE, _I('reciprocal', out=out_ap, in_=out_ap), writes=[outb])

        def retire(bufs, b0):
            for b in bufs:
                if b is not b0:
                    _merge(b0.r, b.r)
                    _merge(b0.r, b.w)

        hT = arena.ap(0, HT_SZ, BF16, "p (k n) -> p k n", k=KC)

        def new_hT_bufs():
            b = arena.buf(0, HT_SZ, "hT")
            return [b] + [Buf("hT%d" % i) for i in range(1, KC)], b

        def inherit(bufs, b0):
            for b in bufs[1:]:
                b.r = dict(b0.r)

        kvnb = [[Buf("kvn") for _ in range(NKB)] for _ in range(KVC)]
        krb = [Buf("kr") for _ in range(NKB)]
        for b in krb:
            b.w = dict(krbuf_z.w)

        ds_xr = [dsem("d_xr%d" % i) for i in range(4)]
        ds_st = [dsem("d_st%d" % i) for i in range(4)]
        ds_gf = dsem("d_gf")
        try:
            A_XN = OFF2
            A_TRIG = OFF2 + 16384
            A_CC = A_TRIG + 8192
            A_SS = A_CC + 2048
            A_SQ = A_SS + 2048
            A_RSTD = A_SQ + 4096
            A_T1 = A_RSTD + 2048
            A_T2 = A_T1 + 2048
            A_KST = A_TRIG
            A_VST = A_T2 + 2048
            assert A_VST + 8192 + 2048 <= ARENA
            kst_cnt = [0]
            ds_kst = [dsem("d_kst0"), dsem("d_kst1")]
            ds_vst = [dsem("d_vst0"), dsem("d_vst1")]
            kscrb = [[Buf("kscr") for _ in range(c.NA)] for _ in range(max(1, H // 4))]
            vscrb = [[Buf("vscr") for _ in range(c.NA)] for _ in range(max(1, H // 4))]
            ds_kh = [dsem("d_kh0"), dsem("d_kh1")]
            ds_vh = [dsem("d_vh0"), dsem("d_vh1")]
            _ckpt('C')
            HG = min(4, H)

            def kv_prod_group(a, g):
                kslab, ksb = load_slab(w_kvb, 0, KVC, g * HG * 128, HG * 128)
                si_ = kst_cnt[0] % 2
                kst_cnt[0] += 1
                kstb = arena.buf(A_KST + si_ * 4096, 4096, "kst")
                kst = arena.ap(A_KST + si_ * 4096, HG * 1024, BF16, "p (h n) -> p h n", h=HG)
                for hl in range(HG):
                    bi = bank_get()
                    mm_group(bi, bk(bi), [(kslab[:, cc_, hl * 128:(hl + 1) * 128],
                                           kvnT_t[:, cc_, a * 512:(a + 1) * 512]) for cc_ in range(KVC)],
                             reads=[ksb] + [kvnb[cc_][a] for cc_ in range(KVC)])
                    if evac_eng() is ACT:
                        k.op(ACT, _I('activation', out=kst[:, hl, :], in_=bk(bi), func=AF.Copy),
                             reads=[bkb(bi)], writes=[kstb] if hl == 0 else (), pwrites=() if hl == 0 else [kstb])
                    else:
                        k.op(DVE, _I('tensor_copy', out=kst[:, hl, :], in_=bk(bi)),
                             reads=[bkb(bi)], writes=[kstb] if hl == 0 else (), pwrites=() if hl == 0 else [kstb])
                    bank_put(bi)
                k.dma(SP, ds_kst[si_], kscr[g * HG:(g + 1) * HG, :, a * 512:(a + 1) * 512].rearrange("h p n -> p h n"),
                      kst, reads=[kstb], writes=[kscrb[g][a]])
                vslab, vsb = load_slab(w_kvb, 0, KVC, H * 128 + g * HG * 128, HG * 128)
                vstb = arena.buf(A_VST + si_ * 4096, 4096, "vst")
                vst = arena.ap(A_VST + si_ * 4096, HG * 1024, BF16, "p (h k d) -> p h k d", h=HG, k=4)
                for q4 in range(4):
                    kt = a * 4 + q4
                    bi = bank_get()
                    mm_group(bi, bk(bi)[:, 0:HG * 128],
                             [(kvnT_t[:, cc_, kt * 128:(kt + 1) * 128], vslab[:, cc_, :]) for cc_ in range(KVC)],
                             reads=[vsb] + [kvnb[cc_][a] for cc_ in range(KVC)])
                    src = bk(bi)[:, 0:HG * 128].rearrange("p (h d) -> p h d", h=HG)
                    if evac_eng() is ACT:
                        k.op(ACT, _I('activation', out=vst[:, :, q4, :], in_=src, func=AF.Copy),
                             reads=[bkb(bi)], writes=[vstb] if q4 == 0 else (), pwrites=() if q4 == 0 else [vstb])
                    else:
                        k.op(DVE, _I('tensor_copy', out=vst[:, :, q4, :], in_=src),
                             reads=[bkb(bi)], writes=[vstb] if q4 == 0 else (), pwrites=() if q4 == 0 else [vstb])
                    bank_put(bi)
                k.dma(SP, ds_vst[si_], vscr[g * HG:(g + 1) * HG, :, a * 512:(a + 1) * 512].rearrange(
                    "h p (k d) -> p h k d", k=4), vst, reads=[vstb], writes=[vscrb[g][a]])

            nrt_tot = S // 128
            A_XNS = [A_XN, A_XN + 8192]
            pres = {0: x_prefetch(x_seq[0:128, :], 128)}
            if nrt_tot > 1:
                pres[1] = x_prefetch(x_seq[128:256, :], 128)
            s1s = {0: rows_stage1(pres[0], A_XNS)}
            for a in range(c.NA):
                hb, hb0 = new_hT_bufs()
                inherit(hb, hb0)
                for rt in range(4):
                    idx = a * 4 + rt
                    if idx + 2 < nrt_tot:
                        pres[idx + 2] = x_prefetch(x_seq[(idx + 2) * 128:(idx + 3) * 128, :], 128)
                    if idx + 1 < nrt_tot:
                        s1s[idx + 1] = rows_stage1(pres[idx + 1], A_XNS)
                    rows_stage2(s1s[idx], hT, hb, rt * 128, 128, c.c_gmix)
                    if rt == 3:
                        early_slab = load_slab(w_kvin, 0, KC, 0, 256)
                    if a >= 1 and rt < H // HG:
                        kv_prod_group(a - 1, rt)
                _ckpt('A%da' % a)
                cct, ccb, sst, ssb = trig_tables(pos_seq[:, a * 512:(a + 1) * 512], A_TRIG, A_CC, A_SS)
                _ckpt('A%db' % a)
                chunk_bank = []
                ncols_tot = KVL + 256
                nchunks = ncols_tot // 128
                ci = 0
                for c0 in range(0, ncols_tot, 256):
                    ncol = min(256, ncols_tot - c0)
                    slab, sbuf_ = early_slab if c0 == 0 else load_slab(w_kvin, 0, KC, c0, ncol)
                    for j in range(ncol // 128):
                        bi = bank_get()
                        mm_group(bi, bk(bi), [(slab[:, kk, j * 128:(j + 1) * 128], hT[:, kk, 0:512]) for kk in range(KC)],
                                 reads=[sbuf_] + hb)
                        chunk_bank.append(bi)
                        ci += 1
                _ckpt('A%dc' % a)
                rb = arena.buf(A_RSTD, 2048, "rstdkv")
                rstd_kv = arena.ap(A_RSTD, 2048, F32)
                st = stat_begin(A_SQ, A_VST + 8192)
                for j in range(KVC):
                    stat_add(st, chunk_bank[j], j == KVC - 1)
                stat_finish(st, KVL, rstd_kv, rb)
                for j in range(KVC):
                    bi = chunk_bank[j]
                    k.op(DVE, _I('scalar_tensor_tensor',
                        out=kvnT_t[:, j, a * 512:(a + 1) * 512], in0=bk(bi), scalar=col(c.c_gkva + j),
                        in1=rstd_kv, op0=ALU.mult, op1=ALU.mult),
                        reads=[bkb(bi), rb, cbuf], writes=[kvnb[j][a]])
                    bank_put(bi)
                _ckpt('A%dd' % a)
                bA, bB = chunk_bank[KVC], chunk_bank[KVC + 1]
                t1b = arena.buf(A_T1, 2048, "t1")
                t1 = arena.ap(A_T1, 2048, F32)[0:64, :]
                t2b = arena.buf(A_T2, 2048, "t2")
                t2 = arena.ap(A_T2, 2048, F32)[0:64, :]
                k.op(DVE, _I('tensor_tensor', out=t1, in0=bk(bA)[0:64, :], in1=cct, op=ALU.mult),
                     reads=[bkb(bA), ccb], writes=[t1b])
                k.op(DVE, _I('tensor_tensor', out=t2, in0=bk(bB)[0:64, :], in1=sst, op=ALU.mult),
                     reads=[bkb(bB), ssb], writes=[t2b])
                k.op(DVE, _I('tensor_tensor', out=krT_t[0:64, a * 512:(a + 1) * 512], in0=t1, in1=t2, op=ALU.add),
                     reads=[t1b, t2b], writes=[krb[a]])
                bank_put(bA)
                bank_put(bB)
                retire(hb, hb0)
                _ckpt('A%d' % a)

            for g in range(H // HG):
                kv_prod_group(c.NA - 1, g)

            P1_XN = OFF4
            P1_CC = OFF4 + 8192
            P1_U = P1_CC + 4608
            P1_YC = P1_U + 4608
            P1_SQ = P1_YC + 2048
            assert P1_SQ + 4096 + 2048 <= ARENA
            P2_CC = OFF4
            P2_SS = OFF4 + 2048
            P2_QN = OFF4 + 4096
            P2_QR = P2_QN + 2048
            P2_RL = P2_QR + 2048
            P2_PT = P2_RL + 2048
            P2_RT = P2_PT + 4096
            P2_ACC = P2_RT + 6144
            P2_HL = P2_ACC + 4096
            assert P2_HL + 2048 <= ARENA
            P3_G = P3_G_OFF
            OFF_MG = OFF3
            P4_XR = OFF2
            P5_XN = OFF2
            P5_ACT = OFF2 + 8192
            PF_G = OFF3
            PF_J = OFF3 + D * 4
            assert PF_J + D * 2 <= ARENA
            xrcnt = [0]
            FG = 8

            for tj in range(c.NT):
                tb0 = tj * 514
                hb, hb0 = new_hT_bufs()
                inherit(hb, hb0)
                srcs = [(x_own[tb0 + rt * 128:tb0 + rt * 128 + 128, :], 128, rt * 128, 128) for rt in range(4)]
                srcs.append((x_own[tb0 + 512:tb0 + 514, :], 2, 512, 2))
                P1_XNS = [P1_XN, P1_XN + 8192]
                pr_ = {0: x_prefetch(srcs[0][0], srcs[0][1]), 1: x_prefetch(srcs[1][0], srcs[1][1])}
                st_ = {0: rows_stage1(pr_[0], P1_XNS)}
                for ri, (src_, nr_, c0_, nc_) in enumerate(srcs):
                    if ri + 2 < len(srcs):
                        pr_[ri + 2] = x_prefetch(srcs[ri + 2][0], srcs[ri + 2][1])
                    if ri + 1 < len(srcs):
                        st_[ri + 1] = rows_stage1(pr_[ri + 1], P1_XNS)
                    rows_stage2(st_[ri], hT, hb, c0_, nc_, c.c_gmix)

                yab0 = arena.buf(OFF2, YA_SZ, "yaT")
                yab = [yab0] + [Buf("ya") for _ in range(CC - 1)]
                inherit(yab, yab0)
                yaT = arena.ap(OFF2, CC * 1024, BF16, "p (k n) -> p k n", k=CC)
                for blk in range(CONV // 256):
                    ccs, us = [], []
                    slab, sbuf_ = load_slab(w_a, 0, KC, CONV + blk * 256, 256)
                    for j in range(2):
                        bi = bank_get()
                        bh = bank_get()
                        mm_group(bi, bk(bi), [(slab[:, kk, j * 128:(j + 1) * 128], hT[:, kk, 0:512]) for kk in range(KC)],
                                 reads=[sbuf_] + hb)
                        mm_group(bh, bk(bh)[:, 0:2], [(slab[:, kk, j * 128:(j + 1) * 128], hT[:, kk, 512:514])
                                                      for kk in range(KC)], reads=[sbuf_] + hb)
                        cb_ = arena.buf(P1_CC + j * 2304, 2304, "ccsb")
                        cs = arena.ap(P1_CC + j * 2304, 2056, F32)
                        k.op(ACT, _I('activation', out=cs[:, 1:513], in_=bk(bi), func=AF.Copy),
                             reads=[bkb(bi)], writes=[cb_])
                        k.op(ACT, _I('activation', out=cs[:, 0:1], in_=bk(bh)[:, 0:1], func=AF.Copy),
                             reads=[bkb(bh)], pwrites=[cb_])
                        k.op(ACT, _I('activation', out=cs[:, 513:514], in_=bk(bh)[:, 1:2], func=AF.Copy),
                             reads=[bkb(bh)], pwrites=[cb_])
                        bank_put(bi)
                        bank_put(bh)
                        ccs.append((cs, cb_))
                    slab, sbuf_ = load_slab(w_a, 0, KC, 2 * CONV + blk * 256, 256)
                    for j in range(2):
                        bi = bank_get()
                        bh = bank_get()
                        mm_group(bi, bk(bi), [(slab[:, kk, j * 128:(j + 1) * 128], hT[:, kk, 0:512]) for kk in range(KC)],
                                 reads=[sbuf_] + hb)
                        mm_group(bh, bk(bh)[:, 0:2], [(slab[:, kk, j * 128:(j + 1) * 128], hT[:, kk, 512:514])
                                                      for kk in range(KC)], reads=[sbuf_] + hb)
                        cs, cb_ = ccs[j]
                        ub = arena.buf(P1_U + j * 2304, 2304, "u")
                        u = arena.ap(P1_U + j * 2304, 2056, F32)
                        k.op(DVE, _I('tensor_tensor', out=u[:, 1:513], in0=bk(bi), in1=cs[:, 1:513],
                                                                                op=ALU.mult),
                             reads=[bkb(bi), cb_], writes=[ub])
                        k.op(DVE, _I('tensor_tensor', out=u[:, 0:1], in0=bk(bh)[:, 0:1],
                                                                                in1=cs[:, 0:1], op=ALU.mult),
                             reads=[bkb(bh), cb_], pwrites=[ub])
                        k.op(DVE, _I('tensor_tensor', out=u[:, 513:514], in0=bk(bh)[:, 1:2],
                                                                                in1=cs[:, 513:514], op=ALU.mult),
                             reads=[bkb(bh), cb_], pwrites=[ub])
                        bank_put(bi)
                        bank_put(bh)
                        us.append((u, ub))
                    slab, sbuf_ = load_slab(w_a, 0, KC, blk * 256, 256)
                    for j in range(2):
                        ch = blk * 2 + j
                        bi = bank_get()
                        mm_group(bi, bk(bi), [(slab[:, kk, j * 128:(j + 1) * 128], hT[:, kk, 0:512]) for kk in range(KC)],
                                 reads=[sbuf_] + hb)
                        u, ub = us[j]
                        ycb = arena.buf(P1_YC, 2048, "yc")
                        yc = arena.ap(P1_YC, 2048, F32)
                        w0 = col(c.c_convw + 0 * CC + ch)
                        w1 = col(c.c_convw + 1 * CC + ch)
                        w2 = col(c.c_convw + 2 * CC + ch)
                        k.op(DVE, _I('tensor_scalar', out=yc, in0=u[:, 1:513], scalar1=w1, scalar2=None,
                                                                         op0=ALU.mult), reads=[ub, cbuf], writes=[ycb])
                        k.op(DVE, _I('scalar_tensor_tensor', out=yc, in0=u[:, 0:512], scalar=w0, in1=yc,
                                                                                op0=ALU.mult, op1=ALU.add),
                             reads=[ub], writes=[ycb])
                        k.op(DVE, _I('scalar_tensor_tensor', out=yc, in0=u[:, 2:514], scalar=w2, in1=yc,
                                                                                op0=ALU.mult, op1=ALU.add),
                             reads=[ub], writes=[ycb])
                        k.op(DVE, _I('tensor_tensor', out=yaT[:, ch, :], in0=bk(bi), in1=yc, op=ALU.mult),
                             reads=[bkb(bi), ycb], writes=[yab[ch]])
                        bank_put(bi)

                qnb0 = arena.buf(OFF3, QN_SZ, "qnT")
                qnb = [qnb0] + [Buf("qn") for _ in range(QC - 1)]
                inherit(qnb, qnb0)
                qnT = arena.ap(OFF3, QN_SZ, BF16, "p (k n) -> p k n", k=QC)
                rqb = arena.buf(OFF_RQ, 2048, "rstdq")
                rstdq = arena.ap(OFF_RQ, 2048, F32)
                st = stat_begin(P1_SQ, P1_SQ + 4096)
                for blk in range(QL // 256):
                    slab, sbuf_ = load_slab(w_a, 0, KC, 3 * CONV + blk * 256, 256)
                    for j in range(2):
                        ch = blk * 2 + j
                        bi = bank_get()
                        mm_group(bi, bk(bi), [(slab[:, kk, j * 128:(j + 1) * 128], hT[:, kk, 0:512]) for kk in range(KC)],
                                 reads=[sbuf_] + hb)
                        scaled_copy(evac_eng(), qnT[:, ch, :], bk(bi), col(c.c_gqa + ch), reads=[bkb(bi), cbuf],
                                    writes=[qnb[ch]])
                        stat_add(st, bi, ch == QC - 1)
                        bank_put(bi)
                stat_finish(st, QL, rstdq, rqb)
                _ckpt('P1')
                cct, ccb, sst, ssb = trig_tables(pos_own[:, tj * 512:(tj + 1) * 512], P2_PT, P2_CC, P2_SS)
                ybb0 = arena.buf(OFF_YB, YB_SZ, "ybT")
                ybb = [ybb0] + [Buf("yb") for _ in range(H - 1)]
                inherit(ybb, ybb0)
                ybT = arena.ap(OFF_YB, YB_SZ, BF16, "p (k n) -> p k n", k=CC)
                KhT = arena.ap(OFF_KH, KH_SZ, BF16)
                Vh = arena.ap(OFF_VH, KH_SZ, BF16, "p (k n) -> p k n", k=NKT)
                ptb = [arena.buf(P2_PT + i * 1024, 1024, "PT") for i in range(4)]
                ptap = [arena.ap(P2_PT + i * 1024, 1024, BF16) for i in range(4)]
                rtb = [arena.buf(P2_RT + i * 2048, 2048, "rt") for i in range(3)]
                rtap = [arena.ap(P2_RT + i * 2048, 2048, F32)[0:64, :] for i in range(3)]
                accb = [arena.buf(P2_ACC + i * 2048, 2048, "acc") for i in range(2)]
                accap = [arena.ap(P2_ACC + i * 2048, 2048, F32) for i in range(2)]
                ahib = arena.buf(P2_HL, 1024, "ahi")
                ahi = arena.ap(P2_HL, 1024, BF16)
                alob = arena.buf(P2_HL + 1024, 1024, "alo")
                alo = arena.ap(P2_HL + 1024, 1024, BF16)
                rlb = arena.buf(P2_RL, 2048, "rl")
                rl = arena.ap(P2_RL, 2048, F32)
                qhnb = [arena.buf(P2_QN + i * 1024, 1024, "qhn") for i in range(2)]
                qhn = [arena.ap(P2_QN + i * 1024, 1024, BF16) for i in range(2)]
                qhrb = [arena.buf(P2_QR + i * 1024, 1024, "qhr") for i in range(2)]
                qhr = [arena.ap(P2_QR + i * 1024, 1024, BF16) for i in range(2)]
                for i in range(2):
                    k.op(DVE, _I('memset', qhr[i], 0.0), writes=[qhrb[i]])
                ptc = 0
                HGc = min(4, H)
                khf = [arena.buf(OFF_KH + i * (KH_SZ // 2), KH_SZ // 2, "khalf") for i in range(2)]
                vhf = [arena.buf(OFF_VH + i * (KH_SZ // 2), KH_SZ // 2, "vhalf") for i in range(2)]
                qs_state = {}

                def emit_kv(h, i):
                    k.dma(SP, ds_kh[i], KhT[:, i * (S // 2):(i + 1) * (S // 2)],
                          kscr[h, :, i * (S // 2):(i + 1) * (S // 2)],
                          reads=[kscrb[h // HGc][a_] for a_ in range(i * c.NA // 2, (i + 1) * c.NA // 2)],
                          writes=[khf[i]])
                    k.dma(SP, ds_vh[i], Vh[:, i * (NKT // 2):(i + 1) * (NKT // 2), :],
                          vscr[h, :, i * (S // 2):(i + 1) * (S // 2)].rearrange("p (k d) -> p k d", d=128),
                          reads=[vscrb[h // HGc][a_] for a_ in range(i * c.NA // 2, (i + 1) * c.NA // 2)],
                          writes=[vhf[i]])

                def emit_q(h):
                    hl = h % 2
                    pi = h % 2
                    if hl == 0:
                        qs_state["slab"] = load_slab(w_qb, 0, QC, (h // 2) * 768, 768)
                    qslab, qsb = qs_state["slab"]
                    cb0 = hl * 384
                    bn = bank_get()
                    mm_group(bn, bk(bn), [(qslab[:, cc_, cb0:cb0 + 128], qnT[:, cc_, :]) for cc_ in range(QC)],
                             reads=[qsb] + qnb)
                    k.op(DVE, _I('scalar_tensor_tensor', out=qhn[pi], in0=bk(bn), scalar=c.SCALE,
                                 in1=rstdq, op0=ALU.mult, op1=ALU.mult),
                         reads=[bkb(bn), rqb], writes=[qhnb[pi]])
                    bank_put(bn)
                    b1 = bank_get()
                    mm_group(b1, bk(b1), [(qslab[:, cc_, cb0 + 128:cb0 + 256], qnT[:, cc_, :]) for cc_ in range(QC)],
                             reads=[qsb] + qnb)
                    b2 = bank_get()
                    mm_group(b2, bk(b2), [(qslab[:, cc_, cb0 + 256:cb0 + 384], qnT[:, cc_, :]) for cc_ in range(QC)],
                             reads=[qsb] + qnb)
                    k.op(DVE, _I('tensor_tensor', out=rtap[0], in0=bk(b1)[0:64, :], in1=cct, op=ALU.mult),
                         reads=[bkb(b1), ccb], writes=[rtb[0]])
                    k.op(DVE, _I('tensor_tensor', out=rtap[1], in0=bk(b2)[0:64, :], in1=sst, op=ALU.mult),
                         reads=[bkb(b2), ssb], writes=[rtb[1]])
                    bank_put(b1)
                    bank_put(b2)
                    k.op(DVE, _I('tensor_tensor', out=rtap[2], in0=rtap[0], in1=rtap[1], op=ALU.add),
                         reads=[rtb[0], rtb[1]], writes=[rtb[2]])
                    k.op(DVE, _I('scalar_tensor_tensor', out=qhr[pi][0:64, :], in0=rtap[2], scalar=c.SCALE,
                                 in1=rstdq[0:64, :], op0=ALU.mult, op1=ALU.mult),
                         reads=[rtb[2], rqb], pwrites=[qhrb[pi]])

                emit_kv(0, 0)
                emit_kv(0, 1)
                emit_q(0)
                for hp in range(H // 2):
                    for hl in range(2):
                        h = hp * 2 + hl
                        pi = h % 2
                        if h + 1 < H:
                            emit_q(h + 1)
                        bo = bank_get()
                        LA = 2
                        pidx_of = {}
                        for kt in range(NKT + LA):
                            if kt < NKT:
                                bs = bank_get()
                                k.op(PE, _I('matmul', bk(bs), lhsT=KhT[:, kt * 128:(kt + 1) * 128],
                                            rhs=qhn[pi], start=True, stop=False),
                                     reads=[khf[kt // (NKT // 2)], qhnb[pi]], writes=[bkb(bs)], inc=False)
                                k.op(PE, _I('matmul', bk(bs), lhsT=krT_t[:, kt * 128:(kt + 1) * 128],
                                            rhs=qhr[pi], start=False, stop=True),
                                     reads=[krb[kt // 4], qhrb[pi]], pwrites=[bkb(bs)], inc=True)
                                pidx = ptc % 4
                                ptc += 1
                                pidx_of[kt] = pidx
                                k.op(ACT, _I('activation', out=ptap[pidx], in_=bk(bs), func=AF.Exp),
                                     reads=[bkb(bs)], writes=[ptb[pidx]])
                                bank_put(bs)
                                ai = kt % 2
                                if kt < 2:
                                    k.op(DVE, _I('tensor_copy', out=accap[ai], in_=ptap[pidx]),
                                         reads=[ptb[pidx]], writes=[accb[ai]])
                                else:
                                    k.op(DVE, _I('tensor_tensor', out=accap[ai], in0=accap[ai], in1=ptap[pidx],
                                                 op=ALU.add), reads=[ptb[pidx]], writes=[accb[ai]])
                            if kt >= LA:
                                kv_ = kt - LA
                                pidx = pidx_of[kv_]
                                k.op(PE, _I('matmul', bk(bo), lhsT=Vh[:, kv_, :], rhs=ptap[pidx],
                                            start=(kv_ == 0), stop=(kv_ == NKT - 1)),
                                     reads=[vhf[kv_ // (NKT // 2)], ptb[pidx]], writes=[bkb(bo)] if kv_ == 0 else (),
                                     pwrites=() if kv_ == 0 else [bkb(bo)], inc=True)
                                if kv_ == NKT // 2 - 1 and h + 1 < H:
                                    emit_kv(h + 1, 0)
                        if h + 1 < H:
                            emit_kv(h + 1, 1)
                        k.op(DVE, _I('tensor_tensor', out=accap[0], in0=accap[0], in1=accap[1], op=ALU.add),
                             reads=[accb[1]], writes=[accb[0]])
                        k.op(DVE, _I('tensor_copy', out=ahi, in_=accap[0]), reads=[accb[0]], writes=[ahib])
                        k.op(DVE, _I('tensor_tensor', out=alo, in0=accap[0], in1=ahi, op=ALU.subtract),
                             reads=[accb[0], ahib], writes=[alob])
                        bl = bank_get()
                        k.op(PE, _I('matmul', bk(bl), lhsT=ones_t[:], rhs=ahi, start=True, stop=False),
                             reads=[onesb, ahib], writes=[bkb(bl)], inc=False)
                        k.op(PE, _I('matmul', bk(bl), lhsT=ones_t[:], rhs=alo, start=False, stop=True),
                             reads=[alob], pwrites=[bkb(bl)], inc=True)
                        k.op(DVE, _I('reciprocal', out=rl, in_=bk(bl)), reads=[bkb(bl)], writes=[rlb])
                        k.op(DVE, _I('tensor_tensor', out=ybT[:, h, :], in0=bk(bo), in1=rl, op=ALU.mult),
                             reads=[bkb(bo), rlb], writes=[ybb[h]])
                        bank_put(bo)
                        bank_put(bl)

                _ckpt('P2')
                mgb0 = arena.buf(OFF_MG, KC * 1024, "mergedT")
                mgb = [mgb0] + [Buf("mg") for _ in range(KC - 1)]
                inherit(mgb, mgb0)
                mgT = arena.ap(OFF_MG, KC * 1024, BF16, "p (k n) -> p k n", k=KC)
                gtb = [arena.buf(P3_G + i * 2048, 2048, "gt") for i in range(8)]
                gtap = [arena.ap(P3_G + i * 2048, 2048, F32) for i in range(8)]
                gi = 0
                for blk in range(D // 256):
                    gts = []
                    for br in range(2):
                        slab, sbuf_ = load_slab(w_gt, 0, KC, br * D + blk * 256, 256)
                        for j in range(2):
                            ch = blk * 2 + j
                            bi = bank_get()
                            mm_group(bi, bk(bi), [(slab[:, kk, j * 128:(j + 1) * 128], hT[:, kk, 0:512]) for kk in range(KC)],
                                     reads=[sbuf_] + hb)
                            g_ = gi % 8
                            gi += 1
                            k.op(ACT, _I('activation', out=gtap[g_], in_=bk(bi), func=AF.Sigmoid,
                                         bias=col(c.c_bgate + br * KC + ch)),
                                 reads=[bkb(bi), cbuf], writes=[gtb[g_]])
                            bank_put(bi)
                            gts.append(g_)
                        slab, sbuf_ = load_slab(w_br, br * CONV, CC, blk * 256, 256)
                        src_bufs = yab if br == 0 else ybb
                        srcT = yaT if br == 0 else ybT
                        for j in range(2):
                            ch = blk * 2 + j
                            g_ = gts[br * 2 + j]
                            bi = bank_get()
                            mm_group(bi, bk(bi), [(slab[:, kk, j * 128:(j + 1) * 128], srcT[:, kk, :]) for kk in range(CC)],
                                     reads=[sbuf_] + src_bufs)
                            k.op(DVE, _I('tensor_tensor', out=gtap[g_], in0=bk(bi), in1=gtap[g_], op=ALU.mult),
                                 reads=[bkb(bi)], writes=[gtb[g_]])
                            if br == 1:
                                ga = gts[j]
                                k.op(DVE, _I('tensor_tensor', out=mgT[:, ch, :], in0=gtap[g_], in1=gtap[ga], op=ALU.add),
                                     reads=[gtb[g_], gtb[ga]], writes=[mgb[ch]])
                            bank_put(bi)

                _ckpt('P3')
                retire(hb, hb0); retire(yab, yab0); retire(ybb, ybb0); retire(qnb, qnb0)
                x1b0 = arena.buf(0, 4 * D * 4, "x1")
                x1b = [[Buf("x1") for _ in range(D // 512)] for _ in range(4)]
                for rt in range(4):
                    for b in x1b[rt]:
                        b.r = dict(x1b0.r)
                x1 = [arena.ap(rt * D * 4, D * 4, F32) for rt in range(4)]
                for cb in range(D // 512):
                    bis = [bank_get() for _ in range(4)]
                    nsl = (KC + 15) // 16
                    for s_ in range(nsl):
                        kc_ = min(16, KC - s_ * 16)
                        slab, sbuf_ = load_slab(w_out, s_ * 16 * 128, kc_, cb * 512, 512)
                        for rt in range(4):
                            for kk in range(kc_):
                                kg = s_ * 16 + kk
                                first = (kg == 0)
                                last = (kg == KC - 1)
                                k.op(PE, _I('matmul', bk(bis[rt]), lhsT=mgT[:, kg, rt * 128:(rt + 1) * 128], rhs=slab[:, kk, :],
                                              start=first, stop=last),
                                     reads=([sbuf_] + mgb) if kk == 0 else (),
                                     writes=[bkb(bis[rt])] if first else (), pwrites=() if first else [bkb(bis[rt])],
                                     inc=(kk == kc_ - 1))
                    for rt in range(4):
                        xi = xrcnt[0] % 4
                        xrcnt[0] += 1
                        xrb = arena.buf(P4_XR + xi * 2048, 2048, "xr")
                        xr = arena.ap(P4_XR + xi * 2048, 2048, F32)
                        k.dma(SP, ds_xr[xi], xr, x_own[tb0 + rt * 128:tb0 + (rt + 1) * 128, cb * 512:(cb + 1) * 512],
                              writes=[xrb])
                        k.op(DVE, _I('tensor_tensor',
                            out=x1[rt][:, cb * 512:(cb + 1) * 512], in0=bk(bis[rt]), in1=xr, op=ALU.add),
                            reads=[bkb(bis[rt]), xrb], writes=[x1b[rt][cb]])
                        bank_put(bis[rt])

                _ckpt('P4')
                retire(mgb, mgb0)
                h2b0 = arena.buf(OFF_MG, KC * 1024, "h2T")
                h2b = [h2b0] + [Buf("h2") for _ in range(KC - 1)]
                inherit(h2b, h2b0)
                h2T = arena.ap(OFF_MG, KC * 1024, BF16, "p (k n) -> p k n", k=KC)
                def ffn_stage1(rt):
                    xo_ = P5_XN if rt % 2 == 0 else P5_ACT
                    xnb = arena.buf(xo_, D * 2, "xn2")
                    xn = arena.ap(xo_, D * 2, BF16)
                    si = 12 + 4 * (rt % 2)
                    ssq = stat_t[:, si:si + 1]
                    std = stat_t[:, si + 1:si + 2]
                    rstd = stat_t[:, si + 2:si + 3]
                    sb_ = statb[si]
                    k.op(ACT, _I('activation', out=xn, in_=x1[rt], func=AF.Square, accum_out=ssq),
                         reads=x1b[rt], writes=[xnb, sb_])
                    k.op(ACT, _I('activation', out=std, in_=ssq, func=AF.Sqrt, bias=col(c.c_eps),
                                 scale=1.0 / D), reads=[sb_, cbuf], pwrites=[sb_])
                    k.op(DVE, _I('reciprocal', out=rstd, in_=std), reads=[sb_], pwrites=[sb_])
                    k.op(DVE, _I('tensor_scalar', out=xn, in0=x1[rt], scalar1=rstd, scalar2=None, op0=ALU.mult),
                         reads=[sb_], writes=[xnb])
                    return (xn, xnb)

                f1 = {0: ffn_stage1(0)}
                for rt in range(4):
                    if rt + 1 < 4:
                        f1[rt + 1] = ffn_stage1(rt + 1)
                    rows_stage2(f1[rt], h2T, h2b, rt * 128, 128, c.c_gffn)
                sgb = [arena.buf(P5_XN + i * 2048, 2048, "sg") for i in range(4)]
                sgap = [arena.ap(P5_XN + i * 2048, 2048, F32) for i in range(4)]
                sgi = 0
                f0 = 0
                while f0 < FC:
                    G = min(FG, FC - f0)
                    acb0 = arena.buf(P5_ACT, FG * 1024, "actT")
                    acb = [acb0] + [Buf("ac") for _ in range(G - 1)]
                    inherit(acb, acb0)
                    acT = arena.ap(P5_ACT, FG * 1024, BF16, "p (k n) -> p k n", k=FG)
                    for blk in range(G // 2):
                        cbase = (f0 + blk * 2) * 128
                        gslab, gsb = load_slab(w_g, 0, KC, cbase, 256)
                        uslab, usb = load_slab(w_u, 0, KC, cbase, 256)
                        for j in range(2):
                            fl = blk * 2 + j
                            bg = bank_get()
                            mm_group(bg, bk(bg), [(gslab[:, kk, j * 128:(j + 1) * 128], h2T[:, kk, :]) for kk in range(KC)],
                                     reads=[gsb] + h2b)
                            bu = bank_get()
                            mm_group(bu, bk(bu), [(uslab[:, kk, j * 128:(j + 1) * 128], h2T[:, kk, :]) for kk in range(KC)],
                                     reads=[usb] + h2b)
                            s_ = sgi % 4
                            sgi += 1
                            k.op(ACT, _I('activation', out=sgap[s_], in_=bk(bg), func=AF.Silu),
                                 reads=[bkb(bg)], writes=[sgb[s_]])
                            bank_put(bg)
                            k.op(DVE, _I('tensor_tensor', out=acT[:, fl, :], in0=bk(bu),
                                                                                               in1=sgap[s_], op=ALU.mult),
                                 reads=[bkb(bu), sgb[s_]], writes=[acb[fl]])
                            bank_put(bu)
                    for cp in range(D // 1024):
                        dslab, dsb = load_slab(w_d, f0 * 128, G, cp * 1024, 1024)
                        for half in range(2):
                            cb = cp * 2 + half
                            for rt in range(4):
                                bi = bank_get()
                                mm_group(bi, bk(bi), [(acT[:, fl, rt * 128:(rt + 1) * 128],
                                                       dslab[:, fl, half * 512:(half + 1) * 512]) for fl in range(G)],
                                         reads=[dsb] + acb)
                                k.op(DVE, _I('tensor_tensor',
                                    out=x1[rt][:, cb * 512:(cb + 1) * 512], in0=bk(bi),
                                    in1=x1[rt][:, cb * 512:(cb + 1) * 512], op=ALU.add),
                                    reads=[bkb(bi)], writes=[x1b[rt][cb]])
                                bank_put(bi)
                    retire(acb, acb0)
                    f0 += G

                _ckpt('P5')
                retire(h2b, h2b0)
                gfb = arena.buf(PF_G, D * 4, "gfin")
                gf = arena.ap(PF_G, D * 4, F32)
                k.dma(SP, ds_gf, gf, gfin_d, writes=[gfb])
                for rt in (2, 3, 0, 1):
                    jb = arena.buf(PF_J, D * 2, "junk")
                    junk = arena.ap(PF_J, D * 2, BF16)
                    si = 20 + 4 * (rt % 2)
                    ssq = stat_t[:, si:si + 1]
                    std = stat_t[:, si + 1:si + 2]
                    rstd = stat_t[:, si + 2:si + 3]
                    sb_ = statb[si]
                    k.op(ACT, _I('activation', out=junk, in_=x1[rt], func=AF.Square,
                                                                               accum_out=ssq),
                         reads=x1b[rt], writes=[jb, sb_])
                    k.op(ACT, _I('activation', out=std, in_=ssq, func=AF.Sqrt, bias=col(c.c_eps),
                                                                       scale=1.0 / D), reads=[sb_, cbuf], pwrites=[sb_])
                    k.op(DVE, _I('reciprocal', out=rstd, in_=std), reads=[sb_], pwrites=[sb_])
                    k.op(DVE, _I('scalar_tensor_tensor', out=x1[rt], in0=x1[rt], scalar=rstd, in1=gf,
                                                                                  op0=ALU.mult, op1=ALU.mult),
                         reads=[sb_, gfb], writes=x1b[rt])
                    r0 = tj * 512 + rt * 128
                    k.dma(SP, ds_st[rt], out_d[r0:r0 + 128, :], x1[rt], reads=x1b[rt])
                for rt in range(4):
                    for b in x1b[rt]:
                        _merge(x1b0.r, b.r)
                        _merge(x1b0.w, b.w)
        except _Stop:
            pass

        for d in all_dsems:
            if d.cnt:
                SP.stream.append(("w", d.sem, d.cnt))
        for en in (PE, ACT, DVE, POOL):
            if en.cnt:
                SP.stream.append(("w", en.sem, en.cnt))

        with nc.Block() as block:
            @block.tensor
            def _(e):
                _replay(e, PE)

            @block.scalar
            def _(e):
                _replay(e, ACT)

            @block.vector
            def _(e):
                _replay(e, DVE)

            @block.gpsimd
            def _(e):
                _replay(e, POOL)

            @block.sync
            def _(e):
                _replay(e, SP)
    return nc


def slab_geometry(c):
    HG = min(4, c.H)
    return {
        "w_a": (c.KC, 256), "w_kvin": (c.KC, 256), "w_gt": (c.KC, 256), "w_qb": (c.QC, 768),
        "w_kvb": (c.KVC, HG * 128), "w_br": (c.CC, 256), "w_out": (min(16, c.KC), 512),
        "w_g": (c.KC, 256), "w_u": (c.KC, 256), "w_d": (8, 1024),
    }


def slabify(W, kcf, ncols):
    W = np.asarray(W, np.float32)
    R, N = W.shape
    blk = kcf * 128
    nr = (R + blk - 1) // blk
    if nr * blk != R:
        Wp = np.zeros((nr * blk, N), np.float32)
        Wp[:R] = W
        W = Wp
    assert N % ncols == 0
    ncb = N // ncols
    out = W.reshape(nr, kcf, 128, ncb, ncols).transpose(0, 3, 2, 1, 4)
    return np.ascontiguousarray(out).reshape(nr, ncb, 128, kcf * ncols)


def host_constants(cfg):
    invf = (ROPE_THETA ** (-np.arange(0, 64, 2, dtype=np.float32) / np.float32(64))).astype(np.float32)
    return invf


def prep_shared(cfg, inp):
    c = cfg
    D, CONV, QL, KVL, H = c.D, c.CONV, c.QL, c.KVL, c.H
    w_in = np.asarray(inp["w_in"])[0]
    o_q = 3 * CONV
    o_kv = o_q + QL
    o_kr = o_kv + KVL
    o_g = o_kr + 64
    kr = w_in[:, o_kr:o_kr + 64]
    kr_sw = np.concatenate([kr[:, 32:], kr[:, :32]], axis=1)
    sh = {}
    sh["w_a"] = np.ascontiguousarray(w_in[:, :o_kv])
    sh["w_kvin"] = np.ascontiguousarray(np.concatenate([w_in[:, o_kv:o_kr], kr, kr_sw, kr_sw, kr], axis=1))
    sh["w_gt"] = np.ascontiguousarray(w_in[:, o_g:o_g + 2 * D])
    wq = np.asarray(inp["w_q_b"])[0].reshape(QL, H, 192)
    qn, qr = wq[:, :, :128], wq[:, :, 128:]
    qr_sw = np.concatenate([qr[:, :, 32:], qr[:, :, :32]], axis=2)
    sh["w_qb"] = np.ascontiguousarray(np.concatenate([qn, qr, qr_sw, qr_sw, qr], axis=2).reshape(QL, H * 384))
    wkv = np.asarray(inp["w_kv_b"])[0].reshape(KVL, H, 256)
    kk = wkv[:, :, :128].reshape(KVL, H * 128)
    vv = wkv[:, :, 128:].reshape(KVL, H * 128)
    sh["w_kvb"] = np.ascontiguousarray(np.concatenate([kk, vv], axis=1))
    sh["w_br"] = np.ascontiguousarray(np.asarray(inp["w_branch"])[0].reshape(2 * CONV, D))
    sh["w_out"] = np.ascontiguousarray(np.asarray(inp["w_out"])[0])
    sh["w_g"] = np.ascontiguousarray(np.asarray(inp["w_ffn_gate"])[0])
    sh["w_u"] = np.ascontiguousarray(np.asarray(inp["w_ffn_up"])[0])
    sh["w_d"] = np.ascontiguousarray(np.asarray(inp["w_ffn_down"])[0])

    geo = slab_geometry(c)
    for name in list(sh.keys()):
        sh[name] = slabify(sh[name], *geo[name])

    def colz(v):
        v = np.asarray(v, dtype=np.float32).reshape(-1, 128)
        return v.T
    cols = np.zeros((128, c.NCOLS), np.float32)
    cols[:, c.c_gmix:c.c_gmix + c.KC] = colz(np.asarray(inp["g_mix"])[0])
    cols[:, c.c_bgate:c.c_bgate + 2 * c.KC] = colz(np.asarray(inp["b_gate"])[0])
    cw = np.asarray(inp["conv_w"])[0]
    for tap in range(3):
        cols[:, c.c_convw + tap * c.CC:c.c_convw + (tap + 1) * c.CC] = colz(cw[tap])
    cols[:, c.c_gqa:c.c_gqa + c.QC] = colz(np.asarray(inp["g_q_a"])[0])
    cols[:, c.c_gkva:c.c_gkva + c.KVC] = colz(np.asarray(inp["g_kv_a"])[0])
    cols[:, c.c_gffn:c.c_gffn + c.KC] = colz(np.asarray(inp["g_ffn"])[0])
    invf = host_constants(c)
    cols[0:32, c.c_invf] = invf
    cols[32:64, c.c_invf] = invf
    cols[0:32, c.c_sign] = -1.0
    cols[32:64, c.c_sign] = 1.0
    cols[:, c.c_eps] = RMS_EPS
    sh["cols"] = cols
    sh["gfin"] = np.ascontiguousarray(np.broadcast_to(np.asarray(inp["g_final"], np.float32)[None, :], (128, D)))
    sh["ident"] = np.eye(128, dtype=np.float32)
    return sh


def prep_core(cfg, x, positions, b, half):
    c = cfg
    S, D, NOWN = c.S, c.D, c.NOWN
    xs = np.ascontiguousarray(x[b])
    start = half * NOWN
    tiles = np.zeros((c.NT, 514, D), np.float32)
    for j in range(c.NT):
        t0 = start + j * 512
        tiles[j, :512] = xs[t0:t0 + 512]
        if t0 - 1 >= 0:
            tiles[j, 512] = xs[t0 - 1]
        if t0 + 512 < S:
            tiles[j, 513] = xs[t0 + 512]
    pos = np.asarray(positions[b], np.int32)
    m = {
        "x_seq": xs,
        "x_own": tiles.reshape(c.NT * 514, D),
        "pos_seq": np.ascontiguousarray(np.broadcast_to(pos[None, :], (64, S))),
        "pos_own": np.ascontiguousarray(np.broadcast_to(pos[None, start:start + NOWN], (64, NOWN))),
    }
    return m


_NC_CACHE = {}


def kernel(x, positions, g_mix, w_in, b_gate, conv_w, g_q_a, w_q_b, g_kv_a, w_kv_b,
           w_branch, w_out, g_ffn, w_ffn_gate, w_ffn_up, w_ffn_down, g_final):
    cfg = Cfg()
    inp = dict(g_mix=g_mix, w_in=w_in, b_gate=b_gate, conv_w=conv_w, g_q_a=g_q_a, w_q_b=w_q_b,
               g_kv_a=g_kv_a, w_kv_b=w_kv_b, w_branch=w_branch, w_out=w_out, g_ffn=g_ffn,
               w_ffn_gate=w_ffn_gate, w_ffn_up=w_ffn_up, w_ffn_down=w_ffn_down, g_final=g_final)
    x = np.asarray(x, np.float32)
    positions = np.asarray(positions)
    B = x.shape[0]
    sh = prep_shared(cfg, inp)
    in_maps = []
    for core in range(8):
        b, half = core // 2, core % 2
        m = prep_core(cfg, x, positions, b, half)
        m.update(sh)
        in_maps.append(m)
    if "nc" not in _NC_CACHE:
        _NC_CACHE["nc"] = build(cfg)
    nc = _NC_CACHE["nc"]
    res = run_bass_kernel_spmd(nc, in_maps, core_ids=list(range(8)))
    out = np.zeros((B, cfg.S, cfg.D), np.float32)
    for core in range(8):
        b, half = core // 2, core % 2
        out[b, half * cfg.NOWN:(half + 1) * cfg.NOWN] = np.asarray(res.results[core]["out"], np.float32)
    return out
```
